# Optimizing a Trainium2 kernel written in Bass

```python
import math
import jax, jax.numpy as jnp
from jax import lax
import numpy as np

D_MODEL = 1024
BATCH = 4
SEQ = 8192
DEPTH = 2

CHUNK = 64
N_MIXERS = 2
N_SSM_LAYERS = (DEPTH + N_MIXERS - 1) // N_MIXERS
N_ATTN_LAYERS = DEPTH // N_MIXERS

SSM_EXPAND = 2
SSM_D_INNER = SSM_EXPAND * D_MODEL
SSM_HEAD_DIM = 64
SSM_HEADS = SSM_D_INNER // SSM_HEAD_DIM
SSM_GROUPS = 8
SSM_HPG = SSM_HEADS // SSM_GROUPS
SSM_STATE = 128
SSM_CONV = 4
SSM_CONV_DIM = SSM_D_INNER + 2 * SSM_GROUPS * SSM_STATE
SSM_IN_DIM = SSM_D_INNER + SSM_CONV_DIM + SSM_HEADS

DIFF_HEAD_DIM = 64
DIFF_HEADS = D_MODEL // (2 * DIFF_HEAD_DIM)
DIFF_V_DIM = 2 * DIFF_HEAD_DIM
Q_BLOCK = 128

FF_DIM = 4 * D_MODEL

DEEPNORM_ALPHA = (2 * DEPTH) ** 0.25
DEEPNORM_BETA = (8 * DEPTH) ** -0.25
EPS = 1e-5

kernel_name = "hybrid_ssd_diffattn_deepnorm_adaln"


def layer_norm(x, g, b):
    xf = x.astype(jnp.float32)
    mu = jnp.mean(xf, axis=-1, keepdims=True)
    xc = xf - mu
    var = jnp.mean(xc * xc, axis=-1, keepdims=True)
    y = xc * lax.rsqrt(var + EPS) * g.astype(jnp.float32) + b.astype(jnp.float32)
    return y.astype(x.dtype)


def causal_depthwise_conv(x, w, b):
    k = w[:, None, :].astype(x.dtype)
    y = lax.conv_general_dilated(x, k, window_strides=(1,), padding=[(SSM_CONV - 1, 0)],
                                 dimension_numbers=("NWC", "WIO", "NWC"),
                                 feature_group_count=x.shape[-1])
    return y + b


def ssd_scan(xdt, a, bm, cm):
    bsz, s = xdt.shape[:2]
    nc = s // CHUNK
    X = xdt.reshape(bsz, nc, CHUNK, SSM_GROUPS, SSM_HPG, SSM_HEAD_DIM)
    A = a.reshape(bsz, nc, CHUNK, SSM_GROUPS, SSM_HPG)
    Bc = bm.reshape(bsz, nc, CHUNK, SSM_GROUPS, SSM_STATE)
    Cc = cm.reshape(bsz, nc, CHUNK, SSM_GROUPS, SSM_STATE)
    A_cs = jnp.cumsum(A, axis=2)
    causal = jnp.tril(jnp.ones((CHUNK, CHUNK), dtype=bool))[:, :, None, None]
    seg = A_cs[:, :, :, None] - A_cs[:, :, None, :]
    decay = jnp.exp(jnp.where(causal, seg, -jnp.inf))
    cb = jnp.einsum('bclgn,bcsgn->bclsg', Cc, Bc)
    y_diag = jnp.einsum('bclsgr,bcsgrp->bclgrp', cb[..., None] * decay, X)
    decay_states = jnp.exp(A_cs[:, :, -1:] - A_cs)
    states = jnp.einsum('bclgn,bclgrp->bcgrpn', Bc, X * decay_states[..., None])
    chunk_decay = jnp.exp(A_cs[:, :, -1])

    def step(h, inp):
        s_c, d_c = inp
        return d_c[..., None, None] * h + s_c, h

    h0 = jnp.zeros_like(states[:, 0])
    _, prev = lax.scan(step, h0, (jnp.moveaxis(states, 1, 0), jnp.moveaxis(chunk_decay, 1, 0)))
    prev = jnp.moveaxis(prev, 0, 1)
    y_off = jnp.einsum('bclgn,bcgrpn->bclgrp', Cc, prev) * jnp.exp(A_cs)[..., None]
    return (y_diag + y_off).reshape(bsz, s, SSM_GROUPS, SSM_HPG, SSM_HEAD_DIM)


def gated_rmsnorm(y, z, w):
    bsz, s, _ = y.shape
    g = (y.astype(jnp.float32) * jax.nn.silu(z.astype(jnp.float32)))
    g = g.reshape(bsz, s, SSM_GROUPS, SSM_D_INNER // SSM_GROUPS)
    g = g * lax.rsqrt(jnp.mean(g * g, axis=-1, keepdims=True) + EPS)
    return (g.reshape(bsz, s, SSM_D_INNER) * w.astype(jnp.float32)).astype(z.dtype)


def ssd_mixer(u, w_in, conv_w, conv_b, dt_bias, a_log, d_skip, norm_w, w_out):
    bsz, s, _ = u.shape
    zxbcdt = u @ w_in
    z = zxbcdt[..., :SSM_D_INNER]
    xbc = zxbcdt[..., SSM_D_INNER:SSM_D_INNER + SSM_CONV_DIM]
    dt = zxbcdt[..., SSM_D_INNER + SSM_CONV_DIM:]
    xbc = jax.nn.silu(causal_depthwise_conv(xbc, conv_w, conv_b))
    xs = xbc[..., :SSM_D_INNER].reshape(bsz, s, SSM_GROUPS, SSM_HPG, SSM_HEAD_DIM)
    bm = xbc[..., SSM_D_INNER:SSM_D_INNER + SSM_GROUPS * SSM_STATE].reshape(bsz, s, SSM_GROUPS, SSM_STATE)
    cm = xbc[..., SSM_D_INNER + SSM_GROUPS * SSM_STATE:].reshape(bsz, s, SSM_GROUPS, SSM_STATE)
    dt = jax.nn.softplus(dt.astype(jnp.float32) + dt_bias.astype(jnp.float32))
    dt = dt.reshape(bsz, s, SSM_GROUPS, SSM_HPG)
    a = -jnp.exp(a_log.astype(jnp.float32)).reshape(SSM_GROUPS, SSM_HPG)
    y = ssd_scan(xs * dt[..., None], dt * a, bm, cm)
    y = y + d_skip.reshape(SSM_GROUPS, SSM_HPG)[:, :, None] * xs
    y = gated_rmsnorm(y.reshape(bsz, s, SSM_D_INNER).astype(u.dtype), z, norm_w)
    return y @ w_out


def diff_attention(u, w_qkv, lq1, lk1, lq2, lk2, subln_w, w_out, lambda_init):
    bsz, s, _ = u.shape
    qkv = u @ w_qkv
    q = qkv[..., :D_MODEL].reshape(bsz, s, DIFF_HEADS, 2, DIFF_HEAD_DIM).transpose(0, 2, 3, 1, 4)
    k = qkv[..., D_MODEL:2 * D_MODEL].reshape(bsz, s, DIFF_HEADS, 2, DIFF_HEAD_DIM).transpose(0, 2, 3, 1, 4)
    v = qkv[..., 2 * D_MODEL:].reshape(bsz, s, DIFF_HEADS, DIFF_V_DIM).transpose(0, 2, 1, 3)
    lam = (jnp.exp(jnp.sum(lq1.astype(jnp.float32) * lk1.astype(jnp.float32)))
           - jnp.exp(jnp.sum(lq2.astype(jnp.float32) * lk2.astype(jnp.float32))) + lambda_init)
    scale = DIFF_HEAD_DIM ** -0.5
    nq = s // Q_BLOCK
    qb = q.reshape(bsz, DIFF_HEADS, 2, nq, Q_BLOCK, DIFF_HEAD_DIM).transpose(3, 0, 1, 2, 4, 5)
    key_chunk = jnp.arange(s) // CHUNK

    def block(args):
        q_blk, idx = args
        sc = jnp.einsum('bhiqd,bhikd->bhiqk', q_blk, k).astype(jnp.float32) * scale
        q_chunk = (idx * Q_BLOCK + jnp.arange(Q_BLOCK)) // CHUNK
        mask = key_chunk[None, :] <= q_chunk[:, None]
        p = jax.nn.softmax(jnp.where(mask, sc, -jnp.inf), axis=-1)
        attn = p[:, :, 0] - lam * p[:, :, 1]
        return jnp.einsum('bhqk,bhkv->bhqv', attn.astype(v.dtype), v)

    o = lax.map(block, (qb, jnp.arange(nq)))
    o = o.transpose(1, 2, 0, 3, 4).reshape(bsz, DIFF_HEADS, s, DIFF_V_DIM).astype(jnp.float32)
    o = o * lax.rsqrt(jnp.mean(o * o, axis=-1, keepdims=True) + EPS)
    o = o * subln_w.astype(jnp.float32) * (1.0 - lambda_init)
    o = o.transpose(0, 2, 1, 3).reshape(bsz, s, DIFF_HEADS * DIFF_V_DIM).astype(u.dtype)
    return o @ w_out


def sq_relu_mlp(u, w1, w2):
    h = jax.nn.relu(u @ w1)
    return (h * h) @ w2


def setup_inputs(seed: int = 0) -> dict:
    key = jax.random.key(seed)
    ks = jax.random.split(key, 32)
    f32 = jnp.float32

    def nrm(k, shape, s):
        return s * jax.random.normal(k, shape, f32)

    NS, NA = N_SSM_LAYERS, N_ATTN_LAYERS
    dt0 = jnp.exp(jax.random.uniform(ks[14], (NS, SSM_HEADS), f32, math.log(1e-3), math.log(1e-1)))
    dt_bias = dt0 + jnp.log(-jnp.expm1(-dt0))
    return {
        "x": nrm(ks[0], (BATCH, SEQ, D_MODEL), 1.0),
        "c": nrm(ks[1], (BATCH, D_MODEL), 1.0),
        "ada_w": nrm(ks[2], (DEPTH, D_MODEL, 6 * D_MODEL), 0.5 * D_MODEL ** -0.5),
        "ada_b": nrm(ks[3], (DEPTH, 6 * D_MODEL), 0.02),
        "ln1_g": 1.0 + nrm(ks[4], (DEPTH, D_MODEL), 0.02),
        "ln1_b": nrm(ks[5], (DEPTH, D_MODEL), 0.02),
        "ln2_g": 1.0 + nrm(ks[6], (DEPTH, D_MODEL), 0.02),
        "ln2_b": nrm(ks[7], (DEPTH, D_MODEL), 0.02),
        "mlp_w1": nrm(ks[8], (DEPTH, D_MODEL, FF_DIM), D_MODEL ** -0.5),
        "mlp_w2": nrm(ks[9], (DEPTH, FF_DIM, D_MODEL), DEEPNORM_BETA * FF_DIM ** -0.5),
        "ssm_w_in": nrm(ks[10], (NS, D_MODEL, SSM_IN_DIM), D_MODEL ** -0.5),
        "ssm_conv_w": nrm(ks[11], (NS, SSM_CONV, SSM_CONV_DIM), SSM_CONV ** -0.5),
        "ssm_conv_b": nrm(ks[12], (NS, SSM_CONV_DIM), 0.02),
        "ssm_dt_bias": dt_bias,
        "ssm_a_log": jnp.log(jax.random.uniform(ks[15], (NS, SSM_HEADS), f32, 1.0, 16.0)),
        "ssm_d": 1.0 + nrm(ks[16], (NS, SSM_HEADS), 0.02),
        "ssm_norm_w": 1.0 + nrm(ks[17], (NS, SSM_D_INNER), 0.02),
        "ssm_w_out": nrm(ks[18], (NS, SSM_D_INNER, D_MODEL), DEEPNORM_BETA * SSM_D_INNER ** -0.5),
        "attn_w_qkv": nrm(ks[19], (NA, D_MODEL, 3 * D_MODEL), D_MODEL ** -0.5),
        "attn_lq1": nrm(ks[20], (NA, DIFF_HEAD_DIM), 0.1),
        "attn_lk1": nrm(ks[21], (NA, DIFF_HEAD_DIM), 0.1),
        "attn_lq2": nrm(ks[22], (NA, DIFF_HEAD_DIM), 0.1),
        "attn_lk2": nrm(ks[23], (NA, DIFF_HEAD_DIM), 0.1),
        "attn_subln_w": 1.0 + nrm(ks[24], (NA, DIFF_V_DIM), 0.02),
        "attn_w_out": nrm(ks[25], (NA, D_MODEL, D_MODEL), DEEPNORM_BETA * D_MODEL ** -0.5),
    }


def reference(x, c, ada_w, ada_b, ln1_g, ln1_b, ln2_g, ln2_b, mlp_w1, mlp_w2,
              ssm_w_in, ssm_conv_w, ssm_conv_b, ssm_dt_bias, ssm_a_log, ssm_d,
              ssm_norm_w, ssm_w_out, attn_w_qkv, attn_lq1, attn_lk1, attn_lq2,
              attn_lk2, attn_subln_w, attn_w_out):
    cond = jax.nn.silu(c)
    for i in range(DEPTH):
        mods = (cond @ ada_w[i] + ada_b[i])[:, None, :]
        sh1, sc1, g1, sh2, sc2, g2 = jnp.split(mods, 6, axis=-1)
        h = x * (1 + sc1) + sh1
        j = i // N_MIXERS
        if i % N_MIXERS == 0:
            y = ssd_mixer(h, ssm_w_in[j], ssm_conv_w[j], ssm_conv_b[j], ssm_dt_bias[j],
                          ssm_a_log[j], ssm_d[j], ssm_norm_w[j], ssm_w_out[j])
        else:
            lambda_init = 0.8 - 0.6 * math.exp(-0.3 * i)
            y = diff_attention(h, attn_w_qkv[j], attn_lq1[j], attn_lk1[j], attn_lq2[j],
                               attn_lk2[j], attn_subln_w[j], attn_w_out[j], lambda_init)
        x = layer_norm(DEEPNORM_ALPHA * x + g1 * y, ln1_g[i], ln1_b[i])
        h = x * (1 + sc2) + sh2
        x = layer_norm(DEEPNORM_ALPHA * x + g2 * sq_relu_mlp(h, mlp_w1[i], mlp_w2[i]), ln2_g[i], ln2_b[i])
    return x
```

```python
import numpy as np, math
from contextlib import ExitStack
import concourse.bass as bass
import concourse.mybir as mybir
from concourse.bass_utils import run_bass_kernel_spmd

F32 = mybir.dt.float32
BF16 = mybir.dt.bfloat16
AF = mybir.ActivationFunctionType
ALU = mybir.AluOpType
AX = mybir.AxisListType

D = 1024
FF = 4096
DI = 2048
NH = 32
NGRP = 8
NST = 128
XBC = 4096
EPS = 1e-5
ALPHA = (2 * 2) ** 0.25
AH = 8
LAMBDA_INIT = 0.8 - 0.6 * math.exp(-0.3 * 1)
NEG = -30000.0
STRICT = True


class Ev:
    __slots__ = ("stream", "idx", "needed", "val", "sem")

    def __init__(self, stream, idx):
        self.stream = stream
        self.idx = idx
        self.needed = False
        self.val = None
        self.sem = None


class Buf:
    __slots__ = ("name", "w", "r")

    def __init__(self, name=""):
        self.name = name
        self.w = None
        self.r = {}


class T:
    __slots__ = ("ap", "b")

    def __init__(self, ap, b=None):
        self.ap = ap
        self.b = b if b is not None else Buf()


ENGS = ("pe", "act", "dve", "pool", "sp")


class Prog:
    def __init__(self, nc, es):
        self.nc = nc
        self.es = es
        self.streams = {e: [] for e in ENGS}
        self.cnt = {e: 0 for e in ENGS}
        self.esem = {e: es.enter_context(nc.semaphore("s_" + e)) for e in ENGS}
        self.dsem = {}
        self.dcnt = {}
        self.waited = {e: {} for e in ENGS}
        self.last = {}

    def op(self, eng, fn, reads=(), writes=(), dma=None):
        deps = []
        for b in reads:
            if b.w is not None:
                deps.append((b.w, 0))
        for b in writes:
            if b.w is not None:
                deps.append((b.w, 1))
            for ev in b.r.values():
                deps.append((ev, 2))
        waits = []
        wd = self.waited[eng]
        for ev, kind in deps:
            st = ev.stream
            if dma is None and st == eng and (eng == "pe" or (kind != 0 and not STRICT)):
                continue
            if wd.get(st, -1) >= ev.idx:
                continue
            wd[st] = ev.idx
            ev.needed = True
            waits.append(ev)
        if dma is not None:
            if dma not in self.dsem:
                self.dsem[dma] = self.es.enter_context(self.nc.semaphore("d_" + dma))
                self.dcnt[dma] = 0
            st = "D:" + dma
            self.dcnt[dma] += 1
            ev = Ev(st, self.dcnt[dma])
            ev.needed = True
        else:
            st = eng
            self.cnt[eng] += 1
            ev = Ev(st, self.cnt[eng])
        self.last[st] = ev
        self.streams[eng].append((waits, fn, ev, dma))
        for b in reads:
            b.r[st] = ev
        for b in writes:
            b.w = ev
            b.r = {}
        return ev

    def barrier(self):
        evs = list(self.last.values())
        for eng in ENGS:
            waits = []
            wd = self.waited[eng]
            for ev in evs:
                if ev.stream == eng:
                    continue
                if wd.get(ev.stream, -1) >= ev.idx:
                    continue
                wd[ev.stream] = ev.idx
                ev.needed = True
                waits.append(ev)
            if waits:
                self.streams[eng].append((waits, None, None, None))

    def flush(self):
        nc = self.nc
        for eng in ENGS:
            v = 0
            dv = {}
            for waits, fn, ev, dma in self.streams[eng]:
                if ev is None:
                    continue
                if dma is not None:
                    ev.val = 16 * ev.idx
                    ev.sem = self.dsem[dma]
                elif ev.needed:
                    v += 1
                    ev.val = v
                    ev.sem = self.esem[eng]
        streams = self.streams

        def emit(e, lst):
            for waits, fn, ev, dma in lst:
                for w in waits:
                    e.wait_ge(w.sem, w.val)
                if fn is None:
                    continue
                ins = fn(e)
                if dma is not None:
                    ins.then_inc(ev.sem, 16)
                elif ev.needed:
                    ins.then_inc(ev.sem, 1)

        with nc.Block() as block:
            @block.sync
            def _(e):
                emit(e, streams["sp"])

            @block.scalar
            def _(e):
                emit(e, streams["act"])

            @block.vector
            def _(e):
                emit(e, streams["dve"])

            @block.gpsimd
            def _(e):
                emit(e, streams["pool"])

            @block.tensor
            def _(e):
                emit(e, streams["pe"])


class Arena:
    def __init__(self, ap32):
        self.ap = ap32
        self.n = ap32.shape[1]
        self.off = 0

    def reset(self, off=0):
        self.off = off

    def f32(self, shape):
        n = int(np.prod(shape[1:]))
        a = self.ap[0:shape[0], self.off:self.off + n]
        self.off += n
        assert self.off <= self.n, ("SBUF arena overflow", self.off, self.n)
        if len(shape) == 3:
            a = a.rearrange("p (a b) -> p a b", a=shape[1])
        return T(a)

    def bf(self, shape):
        n = int(np.prod(shape[1:]))
        n32 = (n + 1) // 2
        a = self.ap[0:shape[0], self.off:self.off + n32].bitcast(BF16)[:, 0:n]
        self.off += n32
        assert self.off <= self.n, ("SBUF arena overflow", self.off, self.n)
        if len(shape) == 3:
            a = a.rearrange("p (a b) -> p a b", a=shape[1])
        return T(a)


def bc(ap2, n):
    return ap2.unsqueeze(2).to_broadcast([ap2.shape[0], ap2.shape[1], n])


def build(S, dbg=False):
    nc = bass.Bass("TRN2", target_bir_lowering=False)
    NT = S // 128
    es = ExitStack()
    ext = {}

    def din(name, shape):
        ext[name] = nc.dram_tensor(name, list(shape), F32, kind="ExternalInput").ap()
        return ext[name]

    x_d = din("x", [S, D])
    c_d = din("c", [D])
    adaw_d = din("ada_w", [2, D, 6 * D])
    adab_d = din("ada_b", [2, 6 * D])
    ln1g_d = din("ln1_g", [2, D]); ln1b_d = din("ln1_b", [2, D])
    ln2g_d = din("ln2_g", [2, D]); ln2b_d = din("ln2_b", [2, D])
    w1_d = din("mlp_w1", [2, D, FF]); w2_d = din("mlp_w2", [2, FF, D])
    win_d = din("ssm_w_in", [D, 6176])
    convw_d = din("ssm_conv_w", [4, XBC]); convb_d = din("ssm_conv_b", [XBC])
    dtb_d = din("ssm_dt_bias", [NH]); alog_d = din("ssm_a_log", [NH]); dsk_d = din("ssm_d", [NH])
    nw_d = din("ssm_norm_w", [DI]); wout_d = din("ssm_w_out", [DI, D])
    wqkv_d = din("attn_w_qkv", [D, 3 * D])
    lq1_d = din("attn_lq1", [64]); lk1_d = din("attn_lk1", [64])
    lq2_d = din("attn_lq2", [64]); lk2_d = din("attn_lk2", [64])
    subw_d = din("attn_subln_w", [128]); awo_d = din("attn_w_out", [D, D])
    cst_d = din("consts", [128, 3 * 128 + 1024])
    out_d = nc.dram_tensor("out", [S, D], F32, kind="ExternalOutput").ap()

    def scratch(name, shape, dt):
        kind = "ExternalOutput" if dbg else "Internal"
        return nc.dram_tensor(name, list(shape), dt, kind=kind).ap()

    MOD = scratch("MOD", [2, 6 * D], F32)
    YF = scratch("YF", [16, 128, S], BF16)
    X1 = scratch("X1", [S, D], F32)
    H2F = scratch("H2F", [8, 128, S], BF16)
    X2 = scratch("X2", [S, D], F32)
    H1F = scratch("H1F", [8, 128, S], BF16)
    OFs = scratch("OFs", [8, 128, S], BF16)
    X3 = scratch("X3", [S, D], F32)
    H4F = scratch("H4F", [8, 128, S], BF16)
    NSUP = max(1, S // 512)
    dbuf = {}

    def DB(name, i=0):
        k = (name, i)
        if k not in dbuf:
            dbuf[k] = Buf(name)
        return dbuf[k]

    big = es.enter_context(nc.sbuf_tensor("big", [128, 53000], F32))
    AR = Arena(big[:, :] if hasattr(big, "__getitem__") else big.ap())
    psum = es.enter_context(nc.psum_tensor("psum", [128, 4096], F32))
    psap = psum[:, :] if hasattr(psum, "__getitem__") else psum.ap()
    PB = [T(psap[:, i * 512:(i + 1) * 512]) for i in range(8)]
    P = Prog(nc, es)

    dq = [0]

    def dma(out, in_, reads, writes, key, eng="sp", **kw):
        P.op(eng, lambda e: e.dma_start(out=out, in_=in_, **kw), reads, writes, dma=key)

    cst = AR.f32([128, 3 * 128])
    dma(cst.ap, cst_d[:, 0:384], [], [cst.b], "cst")
    I32 = cst.ap[:, 0:128]
    U32 = cst.ap[:, 128:256]
    ONES32 = cst.ap[:, 256:384]
    cbf = AR.bf([128, 3 * 128 + 128 + 512])
    P.op("dve", lambda e: e.tensor_copy(out=cbf.ap[:, 0:384], in_=cst.ap[:, 0:384]), [cst.b], [cbf.b])
    P.op("dve", lambda e: e.tensor_scalar(out=cbf.ap[:, 384:512], in0=cst.ap[:, 128:256], scalar1=-1.0,
                                          scalar2=None, op0=ALU.mult), [cst.b], [cbf.b])
    _off = AR.off
    mtmp = AR.f32([128, 512])
    dma(mtmp.ap, cst_d[:, 384:896], [], [mtmp.b], "cst2")
    P.op("dve", lambda e: e.tensor_copy(out=cbf.ap[:, 512:512 + 512], in_=mtmp.ap), [mtmp.b], [cbf.b])
    AR.reset(_off)
    Ibf = cbf.ap[:, 0:128]; Ubf = cbf.ap[:, 128:256]; ONESbf = cbf.ap[:, 256:384]; NEGUbf = cbf.ap[:, 384:512]
    MASKbf = cbf.ap[:, 512:1024]
    CB = cbf.b
    base0 = AR.off
    P.barrier()

    def phase0():
        AR.reset(base0)
        craw = AR.f32([128, 8])
        cond = AR.f32([128, 8])
        dma(craw.ap, c_d.rearrange("(k p) -> p k", p=128), [], [craw.b], "c0", allow_slow_non_contiguous=True)
        P.op("act", lambda e: e.activation(out=cond.ap, in_=craw.ap, func=AF.Silu), [craw.b], [cond.b])
        wst = [AR.f32([128, 8, 512]) for _ in range(2)]
        brow = AR.f32([1, 6 * D])
        orow = AR.f32([1, 6 * D])
        n = 0
        for l in range(2):
            dma(brow.ap, adab_d[l:l + 1, :], [], [brow.b], "brow")
            for cb in range(12):
                w = wst[n % 2]
                dma(w.ap, adaw_d[l, :, cb * 512:(cb + 1) * 512].rearrange("(k p) n -> p k n", p=128),
                    [], [w.b], "adaw%d" % (n % 2))
                ps = PB[n % 2]
                for k in range(8):
                    P.op("pe", lambda e, ps=ps, w=w, k=k: e.matmul(ps.ap[0:1, :], lhsT=cond.ap[:, k:k + 1],
                                                                  rhs=w.ap[:, k, :], start=(k == 0), stop=(k == 7)),
                         [cond.b, w.b], [ps.b])
                P.op("dve", lambda e, ps=ps, cb=cb: e.tensor_tensor(out=orow.ap[:, cb * 512:(cb + 1) * 512],
                                                                    in0=ps.ap[0:1, :],
                                                                    in1=brow.ap[:, cb * 512:(cb + 1) * 512], op=ALU.add),
                     [ps.b, brow.b], [orow.b])
                n += 1
            dma(MOD[l:l + 1, :], orow.ap, [orow.b], [DB("MOD")], "modst")

    def load_w(dst, src, K, N, stg, tag, c0=0):
        nb = 0
        CH = stg[0].ap.shape[1]
        for k in range(K // 128):
            for cs in range(0, N, CH):
                cw = min(CH, N - cs)
                s = stg[nb % len(stg)]
                dma(s.ap[:, 0:cw], src[k * 128:(k + 1) * 128, cs:cs + cw], [], [s.b], "%s%d" % (tag, nb % len(stg)))
                eng = ("pool", "act", "pool")[nb % 3] if False else "pool"
                P.op(eng, lambda e, s=s, k=k, cs=cs, cw=cw: e.tensor_copy(out=dst.ap[:, k, c0 + cs:c0 + cs + cw],
                                                                       in_=s.ap[:, 0:cw]), [s.b], [dst.b])
                nb += 1

    grp = []

    def fence():
        for b, key in grp:
            b.w = P.last["D:" + key]
        del grp[:]

    def load_bc(dst, src_row, n):
        dma(dst.ap, src_row.partition_broadcast(128), [], [dst.b], "bc", allow_slow_non_contiguous=True)
        grp.append((dst.b, "bc"))

    def load_col(dst, src_row):
        dma(dst.ap, src_row.rearrange("(k p) -> p k", p=128), [], [dst.b], "col", allow_slow_non_contiguous=True)
        grp.append((dst.b, "col"))

    def rstd_from(var_ap, out_t, tmp_t, reads, scale=1.0):
        P.op("dve", lambda e: e.tensor_scalar(out=tmp_t.ap, in0=var_ap, scalar1=scale, scalar2=EPS,
                                              op0=ALU.mult, op1=ALU.add), reads, [tmp_t.b])
        P.op("act", lambda e: e.activation(out=tmp_t.ap, in_=tmp_t.ap, func=AF.Sqrt), [tmp_t.b], [tmp_t.b])
        P.op("dve", lambda e: e.reciprocal(out=out_t.ap, in_=tmp_t.ap), [tmp_t.b], [out_t.b])

    def tail1(layer, YFd, yname, KC, wo_src, xin_d, xin_name, X1d, x1name, H2d, h2name):
        AR.reset(base0)
        stg = [AR.f32([128, 512]) for _ in range(2)]
        wo = AR.bf([128, KC, D])
        load_w(wo, wo_src, KC * 128, D, stg, "wst")
        g_bc = AR.f32([128, D]); gam = AR.f32([128, D]); bet = AR.f32([128, D])
        load_bc(g_bc, MOD[layer, 2 * D:3 * D], D)
        load_bc(gam, ln1g_d[layer, :], D)
        load_bc(bet, ln1b_d[layer, :], D)
        sc = AR.f32([128, 8]); sh = AR.f32([128, 8])
        load_col(sc, MOD[layer, 4 * D:5 * D]); load_col(sh, MOD[layer, 3 * D:4 * D])
        fence()
        P.op("dve", lambda e: e.tensor_scalar(out=sc.ap, in0=sc.ap, scalar1=1.0, scalar2=None, op0=ALU.add),
             [sc.b], [sc.b])
        yts = [AR.bf([128, KC, 512]) for _ in range(2)]
        xts = [AR.f32([128, D]) for _ in range(2)]
        uts = [AR.f32([128, D]) for _ in range(2)]
        x1s = [AR.f32([128, D]) for _ in range(2)]
        hfs = [AR.bf([128, 8, 512]) for _ in range(2)]
        st6 = AR.f32([128, 12]); mv = AR.f32([128, 2]); rs = AR.f32([128, 1]); tmp1 = AR.f32([128, 1])
        pend = []
        for su in range(NSUP):
            SW = min(512, S)
            yt = yts[su % 2]
            dma(yt.ap[:, :, 0:SW], YFd[:, :, su * 512:su * 512 + SW].rearrange("k p t -> p k t"),
                [DB(yname, su)], [yt.b], "yt%d" % (su % 2))
            hf = hfs[su % 2]
            for tt in range(SW // 128):
                t = su * 4 + tt
                xt = xts[t % 2]; ut = uts[t % 2]; x1 = x1s[t % 2]
                dma(xt.ap, xin_d[t * 128:(t + 1) * 128, :], [DB(xin_name, su)], [xt.b], "xt%d" % (t % 2))
                pso = [PB[0 + 2 * (t % 2)], PB[1 + 2 * (t % 2)]]
                for hlf in range(2):
                    for k in range(KC):
                        P.op("pe", lambda e, hlf=hlf, k=k, yt=yt, tt=tt, pso=pso: e.matmul(
                            pso[hlf].ap, lhsT=yt.ap[:, k, tt * 128:(tt + 1) * 128],
                            rhs=wo.ap[:, k, hlf * 512:(hlf + 1) * 512], start=(k == 0), stop=(k == KC - 1)),
                            [yt.b, wo.b], [pso[hlf].b])
                for hlf in range(2):
                    P.op("dve", lambda e, hlf=hlf, pso=pso, ut=ut: e.tensor_tensor(
                        out=ut.ap[:, hlf * 512:(hlf + 1) * 512], in0=pso[hlf].ap,
                        in1=g_bc.ap[:, hlf * 512:(hlf + 1) * 512], op=ALU.mult), [pso[hlf].b, g_bc.b], [ut.b])
                def part2(xt=xt, ut=ut, x1=x1, t=t, tt=tt, su=su, hf=hf, SW=SW):
                    ln_block(xt, ut, x1, gam, bet, st6, mv, rs, tmp1)
                    dma(X1d[t * 128:(t + 1) * 128, :], x1.ap, [x1.b], [DB(x1name, su)], "x1st%d" % (t % 2), eng="pool")
                    to_fm(x1, hf, tt, sc, sh)
                    if tt == SW // 128 - 1:
                        dma(H2d[:, :, su * 512:su * 512 + SW].rearrange("k p t -> p k t"), hf.ap[:, :, 0:SW], [hf.b],
                            [DB(h2name, su)], "hfst%d" % (su % 2), eng="pool")
                if pend:
                    pend.pop()()
                pend.append(part2)
        while pend:
            pend.pop()()

    def ln_block(xt, ut, x1, gam, bet, st6, mv, rs, tmp1):
        P.op("dve", lambda e: e.scalar_tensor_tensor(out=ut.ap, in0=xt.ap, scalar=ALPHA, in1=ut.ap,
                                                     op0=ALU.mult, op1=ALU.add), [xt.b, ut.b], [ut.b])
        P.op("dve", lambda e: e.bn_stats(out=st6.ap[:, 0:6], in_=ut.ap[:, 0:512]), [ut.b], [st6.b])
        P.op("dve", lambda e: e.bn_stats(out=st6.ap[:, 6:12], in_=ut.ap[:, 512:1024]), [ut.b], [st6.b])
        P.op("dve", lambda e: e.bn_aggr(out=mv.ap, in_=st6.ap), [st6.b], [mv.b])
        rstd_from(mv.ap[:, 1:2], rs, tmp1, [mv.b])
        P.op("dve", lambda e: e.tensor_scalar(out=x1.ap, in0=ut.ap, scalar1=mv.ap[:, 0:1], scalar2=rs.ap[:, 0:1],
                                              op0=ALU.subtract, op1=ALU.mult), [ut.b, mv.b, rs.b], [x1.b])
        P.op("pool", lambda e: e.tensor_tensor(out=x1.ap, in0=x1.ap, in1=gam.ap, op=ALU.mult), [x1.b, gam.b], [x1.b])
        P.op("pool", lambda e: e.tensor_tensor(out=x1.ap, in0=x1.ap, in1=bet.ap, op=ALU.add), [x1.b, bet.b], [x1.b])

    def to_fm(x1, hf, tt, sc, sh):
        for c in range(8):
            ps = PB[4 + (c % 4)]
            P.op("pe", lambda e, c=c, ps=ps: e.transpose(out=ps.ap[:, 0:128], in_=x1.ap[:, c * 128:(c + 1) * 128],
                                                         identity=I32), [x1.b, cst.b], [ps.b])
            P.op("act", lambda e, c=c, ps=ps: e.activation(out=hf.ap[:, c, tt * 128:(tt + 1) * 128], in_=ps.ap[:, 0:128],
                                                           func=AF.Identity, bias=sh.ap[:, c:c + 1],
                                                           scale=sc.ap[:, c:c + 1]), [ps.b, sc.b, sh.b], [hf.b])

    def tail2(layer, H2d, h2name, X1d, x1name, XOd, xoname, HNd=None, hnname=None):
        AR.reset(base0)
        stg = [AR.f32([128, 256]) for _ in range(2)]
        w1 = AR.bf([128, 8, FF]); w2 = AR.bf([128, 32, D])
        load_w(w1, w1_d[layer], D, FF, stg, "wst")
        load_w(w2, w2_d[layer], FF, D, stg, "wst")
        g_bc = AR.f32([128, D]); gam = AR.f32([128, D]); bet = AR.f32([128, D])
        load_bc(g_bc, MOD[layer, 5 * D:6 * D], D)
        load_bc(gam, ln2g_d[layer, :], D)
        load_bc(bet, ln2b_d[layer, :], D)
        sc = AR.f32([128, 8]); sh = AR.f32([128, 8])
        if HNd is not None:
            load_col(sc, MOD[layer + 1, 1 * D:2 * D]); load_col(sh, MOD[layer + 1, 0:D])
        fence()
        if HNd is not None:
            P.op("dve", lambda e: e.tensor_scalar(out=sc.ap, in0=sc.ap, scalar1=1.0, scalar2=None, op0=ALU.add),
                 [sc.b], [sc.b])
        SW = min(256, S)
        h2s = [AR.bf([128, 8, SW]) for _ in range(2)]
        hff = AR.bf([128, 32, SW])
        rts = [AR.f32([128, SW]) for _ in range(2)]
        xts = [AR.f32([128, D]) for _ in range(2)]
        uts = [AR.f32([128, D]) for _ in range(2)]
        x1s = [AR.f32([128, D]) for _ in range(2)]
        hfs = [AR.bf([128, 8, SW]) for _ in range(1)]
        st6 = AR.f32([128, 12]); mv = AR.f32([128, 2]); rs = AR.f32([128, 1]); tmp1 = AR.f32([128, 1])
        nsu = S // SW
        pend = []
        for su in range(nsu):
            dsu = (su * SW) // 512
            h2 = h2s[su % 2]
            dma(h2.ap, H2d[:, :, su * SW:(su + 1) * SW].rearrange("k p t -> p k t"), [DB(h2name, dsu)], [h2.b],
                "h2%d" % (su % 2))
            for fc in range(32):
                ps = PB[fc % 4]
                for k in range(8):
                    P.op("pe", lambda e, fc=fc, k=k, ps=ps, h2=h2: e.matmul(
                        ps.ap[:, 0:SW], lhsT=w1.ap[:, k, fc * 128:(fc + 1) * 128], rhs=h2.ap[:, k, :],
                        start=(k == 0), stop=(k == 7)), [w1.b, h2.b], [ps.b])
                rt = rts[fc % 2]
                P.op("act", lambda e, ps=ps, rt=rt: e.activation(out=rt.ap, in_=ps.ap[:, 0:SW], func=AF.Relu),
                     [ps.b], [rt.b])
                P.op("pool", lambda e, rt=rt, fc=fc: e.tensor_tensor(out=hff.ap[:, fc, :], in0=rt.ap, in1=rt.ap,
                                                                    op=ALU.mult), [rt.b], [hff.b])
            hf = hfs[0]
            for tt in range(SW // 128):
                t = su * (SW // 128) + tt
                xt = xts[t % 2]; ut = uts[t % 2]; x1 = x1s[t % 2]
                dma(xt.ap, X1d[t * 128:(t + 1) * 128, :], [DB(x1name, dsu)], [xt.b], "xt%d" % (t % 2))
                pso = [PB[4 + 2 * (t % 2)], PB[5 + 2 * (t % 2)]]
                for hlf in range(2):
                    for fc in range(32):
                        P.op("pe", lambda e, hlf=hlf, fc=fc, tt=tt, pso=pso: e.matmul(
                            pso[hlf].ap, lhsT=hff.ap[:, fc, tt * 128:(tt + 1) * 128],
                            rhs=w2.ap[:, fc, hlf * 512:(hlf + 1) * 512], start=(fc == 0), stop=(fc == 31)),
                            [hff.b, w2.b], [pso[hlf].b])
                for hlf in range(2):
                    P.op("dve", lambda e, hlf=hlf, pso=pso, ut=ut: e.tensor_tensor(
                        out=ut.ap[:, hlf * 512:(hlf + 1) * 512], in0=pso[hlf].ap,
                        in1=g_bc.ap[:, hlf * 512:(hlf + 1) * 512], op=ALU.mult), [pso[hlf].b, g_bc.b], [ut.b])
                def part2(xt=xt, ut=ut, x1=x1, t=t, tt=tt, su=su, hf=hf, dsu=dsu):
                    ln_block(xt, ut, x1, gam, bet, st6, mv, rs, tmp1)
                    dma(XOd[t * 128:(t + 1) * 128, :], x1.ap, [x1.b], [DB(xoname, dsu)], "x1st%d" % (t % 2), eng="pool")
                    if HNd is not None:
                        to_fm_pb(x1, hf, tt, sc, sh)
                        if tt == SW // 128 - 1:
                            dma(HNd[:, :, su * SW:(su + 1) * SW].rearrange("k p t -> p k t"), hf.ap, [hf.b],
                                [DB(hnname, dsu)], "hfst0", eng="pool")
                if pend:
                    pend.pop()()
                pend.append(part2)
        while pend:
            pend.pop()()

    def to_fm_pb(x1, hf, tt, sc, sh):
        for c in range(8):
            ps = PB[c % 4]
            P.op("pe", lambda e, c=c, ps=ps: e.transpose(out=ps.ap[:, 256:384], in_=x1.ap[:, c * 128:(c + 1) * 128],
                                                         identity=I32), [x1.b, cst.b], [ps.b])
            P.op("act", lambda e, c=c, ps=ps: e.activation(out=hf.ap[:, c, tt * 128:(tt + 1) * 128],
                                                           in_=ps.ap[:, 256:384], func=AF.Identity,
                                                           bias=sh.ap[:, c:c + 1], scale=sc.ap[:, c:c + 1]),
                 [ps.b, sc.b, sh.b], [hf.b])

    def phaseA():
        AR.reset(base0)
        stg = [AR.f32([128, 512]) for _ in range(2)]
        win = AR.bf([128, 8, 6176])
        load_w(win, win_d, D, 6176, stg, "wst")
        sc = AR.f32([128, 8]); sh = AR.f32([128, 8])
        load_col(sc, MOD[0, D:2 * D]); load_col(sh, MOD[0, 0:D])
        cw = AR.f32([128, 4, 32])
        cbias = AR.f32([128, 32])
        for k in range(4):
            dma(cw.ap[:, k, :], convw_d[k, :].rearrange("(c p) -> p c", p=128), [], [cw.b], "col",
                allow_slow_non_contiguous=True)
        grp.append((cw.b, "col"))
        load_col(cbias, convb_d)
        dtb = AR.f32([128, NH]); Abc = AR.f32([128, NH]); Dbc = AR.f32([128, NH]); nwbc = AR.f32([128, DI])
        load_bc(dtb, dtb_d, NH); load_bc(Abc, alog_d, NH); load_bc(Dbc, dsk_d, NH); load_bc(nwbc, nw_d, DI)
        fence()
        P.op("dve", lambda e: e.tensor_scalar(out=sc.ap, in0=sc.ap, scalar1=1.0, scalar2=None, op0=ALU.add),
             [sc.b], [sc.b])
        P.op("act", lambda e: e.activation(out=Abc.ap, in_=Abc.ap, func=AF.Exp), [Abc.b], [Abc.b])
        P.op("dve", lambda e: e.tensor_scalar(out=Abc.ap, in0=Abc.ap, scalar1=-1.0, scalar2=None, op0=ALU.mult),
             [Abc.b], [Abc.b])
        halo = AR.f32([128, 32, 3])
        P.op("pool", lambda e: e.memset(halo.ap, 0.0), [], [halo.b])
        h32 = AR.f32([128, 8, 256]); hbf = AR.bf([128, 8, 256])
        P.op("pool", lambda e: e.memset(h32.ap, 0.0), [], [h32.b])
        P.op("pool", lambda e: e.memset(hbf.ap, 0.0), [], [hbf.b])
        SW = min(256, S)
        xts = [AR.f32([128, D]) for _ in range(2)]
        hF = AR.bf([128, 8, SW])
        raws = [AR.f32([128, SW + 3]) for _ in range(2)]
        accs = [AR.f32([128, SW]) for _ in range(2)]
        xbcF = AR.bf([128, 32, SW])
        dtr = AR.f32([128, NH]); dtt = AR.f32([128, NH]); a_t = AR.f32([128, NH]); a_bf = AR.bf([128, NH])
        eA = AR.f32([128, NH]); cdb = AR.f32([128, NH]); w2t = AR.f32([128, NH])
        P1 = AR.bf([128, 8, 128]); abc = AR.bf([128, 8, 128])
        LT = AR.bf([128, 8, 128]); MT = AR.bf([128, 8, 128])
        xTs = AR.bf([128, 512]); Xm = AR.bf([128, 512]); Xd = AR.bf([128, 512]); sk = AR.f32([128, 512])
        yo = AR.f32([128, 512]); yy = AR.f32([128, 512]); sz = AR.f32([128, 512]); gg = AR.f32([128, 512])
        junk = AR.f32([128, 256]); ss = AR.f32([128, 2]); rs2 = AR.f32([128, 2]); tmp2 = AR.f32([128, 2])
        yn = AR.bf([128, 512]); BTs = AR.bf([128, 2, 128]); htmp = AR.f32([128, 512])
        yfs = [AR.bf([128, 16, SW]) for _ in range(1)]
        for su in range(S // SW):
            for tt in range(SW // 128):
                t = su * (SW // 128) + tt
                xt = xts[t % 2]
                dma(xt.ap, x_d[t * 128:(t + 1) * 128, :], [], [xt.b], "xt%d" % (t % 2))
                to_fm(xt, hF, tt, sc, sh)
            for ct in range(32):
                ps = PB[ct % 2]
                col0 = DI + ct * 128
                for k in range(8):
                    P.op("pe", lambda e, k=k, ps=ps, col0=col0: e.matmul(
                        ps.ap[:, 0:SW], lhsT=win.ap[:, k, col0:col0 + 128], rhs=hF.ap[:, k, :],
                        start=(k == 0), stop=(k == 7)), [win.b, hF.b], [ps.b])
                raw = raws[ct % 2]; acc = accs[ct % 2]
                P.op("act", lambda e, ps=ps, raw=raw: e.activation(out=raw.ap[:, 3:3 + SW], in_=ps.ap[:, 0:SW],
                                                                  func=AF.Identity), [ps.b], [raw.b])
                P.op("pool", lambda e, raw=raw, ct=ct: e.tensor_copy(out=raw.ap[:, 0:3], in_=halo.ap[:, ct, :]),
                     [halo.b], [raw.b])
                P.op("pool", lambda e, raw=raw, ct=ct: e.tensor_copy(out=halo.ap[:, ct, :], in_=raw.ap[:, SW:SW + 3]),
                     [raw.b], [halo.b])
                P.op("dve", lambda e, raw=raw, acc=acc, ct=ct: e.tensor_scalar(
                    out=acc.ap, in0=raw.ap[:, 0:SW], scalar1=cw.ap[:, 0, ct:ct + 1], scalar2=cbias.ap[:, ct:ct + 1],
                    op0=ALU.mult, op1=ALU.add), [raw.b, cw.b, cbias.b], [acc.b])
                for k in range(1, 4):
                    P.op("dve", lambda e, raw=raw, acc=acc, ct=ct, k=k: e.scalar_tensor_tensor(
                        out=acc.ap, in0=raw.ap[:, k:k + SW], scalar=cw.ap[:, k, ct:ct + 1], in1=acc.ap,
                        op0=ALU.mult, op1=ALU.add), [raw.b, cw.b, acc.b], [acc.b])
                P.op("act", lambda e, acc=acc, ct=ct: e.activation(out=xbcF.ap[:, ct, :], in_=acc.ap, func=AF.Silu),
                     [acc.b], [xbcF.b])
            yf = yfs[0]
            for tt in range(SW // 128):
                tk = slice(tt * 128, (tt + 1) * 128)
                psm = PB[7]
                for k in range(8):
                    P.op("pe", lambda e, k=k, tk=tk: e.matmul(psm.ap[:, 0:NH], lhsT=hF.ap[:, k, tk],
                                                              rhs=win.ap[:, k, 6144:6176], start=(k == 0), stop=(k == 7)),
                         [hF.b, win.b], [psm.b])
                P.op("dve", lambda e: e.tensor_tensor(out=dtr.ap, in0=psm.ap[:, 0:NH], in1=dtb.ap, op=ALU.add),
                     [psm.b, dtb.b], [dtr.b])
                P.op("act", lambda e: e.activation(out=dtr.ap, in_=dtr.ap, func=AF.Exp), [dtr.b], [dtr.b])
                P.op("act", lambda e: e.activation(out=dtt.ap, in_=dtr.ap, func=AF.Ln, bias=1.0), [dtr.b], [dtt.b])
                P.op("dve", lambda e: e.tensor_tensor(out=a_t.ap, in0=dtt.ap, in1=Abc.ap, op=ALU.mult),
                     [dtt.b, Abc.b], [a_t.b])
                P.op("dve", lambda e: e.tensor_copy(out=a_bf.ap, in_=a_t.ap), [a_t.b], [a_bf.b])
                P.op("pe", lambda e: e.matmul(psm.ap[:, 32:64], lhsT=U32, rhs=a_t.ap, start=True, stop=True),
                     [a_t.b, cst.b], [psm.b])
                P.op("pe", lambda e: e.matmul(psm.ap[:, 64:96], lhsT=ONES32, rhs=a_t.ap, start=True, stop=True),
                     [a_t.b, cst.b], [psm.b])
                P.op("act", lambda e: e.activation(out=eA.ap, in_=psm.ap[:, 32:64], func=AF.Exp), [psm.b], [eA.b])
                P.op("act", lambda e: e.activation(out=cdb.ap, in_=psm.ap[:, 64:96], func=AF.Exp), [psm.b], [cdb.b])
                for gp in range(4):
                    hs = slice(gp * 8, gp * 8 + 8)
                    P.op("pool", lambda e, hs=hs: e.tensor_tensor(
                        out=P1.ap, in0=bc(a_bf.ap[:, hs], 128), in1=Ubf.unsqueeze(1).to_broadcast([128, 8, 128]),
                        op=ALU.mult), [a_bf.b, CB], [P1.b])
                    P.op("pool", lambda e, hs=hs: e.tensor_copy(out=abc.ap, in_=bc(a_bf.ap[:, hs], 128)),
                         [a_bf.b], [abc.b])
                    seg = [PB[2], PB[3]]
                    for hh in range(2):
                        cs = slice(hh * 4, hh * 4 + 4)
                        P.op("pe", lambda e, hh=hh, cs=cs: e.matmul(seg[hh].ap, lhsT=ONESbf, rhs=P1.ap[:, cs, :],
                                                                    start=True, stop=False), [P1.b, CB], [seg[hh].b])
                        P.op("pe", lambda e, hh=hh, cs=cs: e.matmul(seg[hh].ap, lhsT=NEGUbf, rhs=abc.ap[:, cs, :],
                                                                    start=False, stop=False), [abc.b, CB], [seg[hh].b])
                        P.op("pe", lambda e, hh=hh: e.matmul(seg[hh].ap, lhsT=Ibf,
                                                             rhs=MASKbf[:, 0:512],
                                                             start=False, stop=True), [CB], [seg[hh].b])
                        P.op("act", lambda e, hh=hh, cs=cs: e.activation(out=LT.ap[:, cs, :], in_=seg[hh].ap,
                                                                         func=AF.Exp), [seg[hh].b], [LT.b])
                    pcb = PB[4]
                    for g2 in range(2):
                        g = gp * 2 + g2
                        P.op("pe", lambda e, g=g, g2=g2, tk=tk: e.matmul(
                            pcb.ap[:, g2 * 128:(g2 + 1) * 128], lhsT=xbcF.ap[:, 16 + g, tk], rhs=xbcF.ap[:, 24 + g, tk],
                            start=True, stop=True), [xbcF.b], [pcb.b])
                    for g2 in range(2):
                        P.op("dve", lambda e, g2=g2: e.tensor_tensor(
                            out=MT.ap[:, g2 * 4:(g2 + 1) * 4, :], in0=LT.ap[:, g2 * 4:(g2 + 1) * 4, :],
                            in1=pcb.ap[:, g2 * 128:(g2 + 1) * 128].unsqueeze(1).to_broadcast([128, 4, 128]),
                            op=ALU.mult), [LT.b, pcb.b], [MT.b])
                    pxt = PB[5]
                    pxt_bf = pxt.ap.bitcast(BF16)
                    for c4 in range(4):
                        ct = gp * 4 + c4
                        P.op("pe", lambda e, ct=ct, c4=c4, tk=tk: e.transpose(
                            out=pxt_bf[:, c4 * 128:(c4 + 1) * 128], in_=xbcF.ap[:, ct, tk], identity=Ibf),
                            [xbcF.b, CB], [pxt.b])
                    P.op("act", lambda e: e.activation(out=xTs.ap, in_=pxt_bf[:, 0:512], func=AF.Identity),
                         [pxt.b], [xTs.b])
                    x3 = xTs.ap.rearrange("p (r d) -> p r d", r=8)
                    P.op("dve", lambda e, hs=hs: e.tensor_tensor(out=w2t.ap[:, hs], in0=dtt.ap[:, hs],
                                                                 in1=LT.ap[:, :, 127], op=ALU.mult),
                         [dtt.b, LT.b], [w2t.b])
                    P.op("pool", lambda e, hs=hs: e.tensor_tensor(
                        out=Xm.ap.rearrange("p (r d) -> p r d", r=8), in0=x3, in1=bc(dtt.ap[:, hs], 64), op=ALU.mult),
                        [xTs.b, dtt.b], [Xm.b])
                    P.op("pool", lambda e, hs=hs: e.tensor_tensor(
                        out=Xd.ap.rearrange("p (r d) -> p r d", r=8), in0=x3, in1=bc(w2t.ap[:, hs], 64), op=ALU.mult),
                        [xTs.b, w2t.b], [Xd.b])
                    P.op("pool", lambda e, hs=hs: e.tensor_tensor(
                        out=sk.ap.rearrange("p (r d) -> p r d", r=8), in0=x3, in1=bc(Dbc.ap[:, hs], 64), op=ALU.mult),
                        [xTs.b, Dbc.b], [sk.b])
                    pyd = PB[6]; pyo = PB[0]
                    for r in range(8):
                        P.op("pe", lambda e, r=r: e.matmul(pyd.ap[:, r * 64:(r + 1) * 64], lhsT=MT.ap[:, r, :],
                                                           rhs=Xm.ap[:, r * 64:(r + 1) * 64], start=True, stop=True),
                             [MT.b, Xm.b], [pyd.b])
                    for g2 in range(2):
                        g = gp * 2 + g2
                        P.op("pe", lambda e, g=g, g2=g2, tk=tk: e.matmul(
                            pyo.ap[:, g2 * 256:(g2 + 1) * 256], lhsT=xbcF.ap[:, 24 + g, tk], rhs=hbf.ap[:, g, :],
                            start=True, stop=True), [xbcF.b, hbf.b], [pyo.b])
                    P.op("dve", lambda e, hs=hs: e.tensor_tensor(
                        out=yo.ap.rearrange("p (r d) -> p r d", r=8), in0=pyo.ap.rearrange("p (r d) -> p r d", r=8),
                        in1=bc(eA.ap[:, hs], 64), op=ALU.mult), [pyo.b, eA.b], [yo.b])
                    P.op("pool", lambda e: e.tensor_tensor(out=yo.ap, in0=yo.ap, in1=sk.ap, op=ALU.add),
                         [yo.b, sk.b], [yo.b])
                    P.op("dve", lambda e: e.tensor_tensor(out=yy.ap, in0=pyd.ap, in1=yo.ap, op=ALU.add),
                         [pyd.b, yo.b], [yy.b])
                    pz = PB[1]
                    for k in range(8):
                        P.op("pe", lambda e, k=k, gp=gp, tk=tk: e.matmul(
                            pz.ap, lhsT=hF.ap[:, k, tk], rhs=win.ap[:, k, gp * 512:(gp + 1) * 512],
                            start=(k == 0), stop=(k == 7)), [hF.b, win.b], [pz.b])
                    P.op("act", lambda e: e.activation(out=sz.ap, in_=pz.ap, func=AF.Silu), [pz.b], [sz.b])
                    P.op("dve", lambda e: e.tensor_tensor(out=gg.ap, in0=yy.ap, in1=sz.ap, op=ALU.mult),
                         [yy.b, sz.b], [gg.b])
                    for g2 in range(2):
                        P.op("act", lambda e, g2=g2: e.activation(out=junk.ap, in_=gg.ap[:, g2 * 256:(g2 + 1) * 256],
                                                                  func=AF.Square, accum_out=ss.ap[:, g2:g2 + 1]),
                             [gg.b], [junk.b, ss.b])
                    rstd_from(ss.ap, rs2, tmp2, [ss.b], scale=1.0 / 256)
                    for g2 in range(2):
                        c0 = gp * 512 + g2 * 256
                        P.op("dve", lambda e, g2=g2, c0=c0: e.scalar_tensor_tensor(
                            out=yn.ap[:, g2 * 256:(g2 + 1) * 256], in0=gg.ap[:, g2 * 256:(g2 + 1) * 256],
                            scalar=rs2.ap[:, g2:g2 + 1], in1=nwbc.ap[:, c0:c0 + 256], op0=ALU.mult, op1=ALU.mult),
                            [gg.b, rs2.b, nwbc.b], [yn.b])
                    for c4 in range(4):
                        ct = gp * 4 + c4
                        P.op("pe", lambda e, c4=c4: e.transpose(out=pxt_bf[:, 512 + c4 * 128:512 + (c4 + 1) * 128],
                                                                in_=yn.ap[:, c4 * 128:(c4 + 1) * 128], identity=Ibf),
                             [yn.b, CB], [pxt.b])
                    P.op("act", lambda e, gp=gp, tk=tk, yf=yf: e.activation(
                        out=yf.ap[:, gp * 4:(gp + 1) * 4, tk],
                        in_=pxt_bf[:, 512:1024].rearrange("p (c t) -> p c t", c=4), func=AF.Identity),
                        [pxt.b], [yf.b])
                    pbt = PB[7]
                    pbt_bf = pbt.ap.bitcast(BF16)
                    for g2 in range(2):
                        g = gp * 2 + g2
                        P.op("pe", lambda e, g=g, g2=g2, tk=tk: e.transpose(
                            out=pbt_bf[:, 256 + g2 * 128:256 + (g2 + 1) * 128], in_=xbcF.ap[:, 16 + g, tk],
                            identity=Ibf), [xbcF.b, CB], [pbt.b])
                    P.op("act", lambda e: e.activation(out=BTs.ap, in_=pbt_bf[:, 256:512].rearrange(
                        "p (g n) -> p g n", g=2), func=AF.Identity), [pbt.b], [BTs.b])
                    pS = PB[0]
                    for g2 in range(2):
                        P.op("pe", lambda e, g2=g2: e.matmul(pS.ap[:, g2 * 256:(g2 + 1) * 256], lhsT=BTs.ap[:, g2, :],
                                                             rhs=Xd.ap[:, g2 * 256:(g2 + 1) * 256], start=True, stop=True),
                             [BTs.b, Xd.b], [pS.b])
                    hv = h32.ap[:, gp * 2:gp * 2 + 2, :].rearrange("p g (r d) -> p (g r) d", r=4)
                    P.op("pool", lambda e, hs=hs, hv=hv: e.tensor_tensor(
                        out=htmp.ap.rearrange("p (r d) -> p r d", r=8), in0=hv, in1=bc(cdb.ap[:, hs], 64), op=ALU.mult),
                        [h32.b, cdb.b], [htmp.b])
                    P.op("dve", lambda e, gp=gp: e.tensor_tensor(
                        out=h32.ap[:, gp * 2:gp * 2 + 2, :].rearrange("p g n -> p (g n)"), in0=pS.ap, in1=htmp.ap,
                        op=ALU.add), [pS.b, htmp.b], [h32.b])
                    P.op("act", lambda e, gp=gp: e.activation(out=hbf.ap[:, gp * 2:gp * 2 + 2, :],
                                                             in_=h32.ap[:, gp * 2:gp * 2 + 2, :], func=AF.Identity),
                         [h32.b], [hbf.b])
            dma(YF[:, :, su * SW:(su + 1) * SW].rearrange("k p t -> p k t"), yf.ap, [yf.b], [DB("YF", (su * SW) // 512)],
                "yfst0", eng="pool")

    def phaseC():
        AR.reset(base0)
        stg = [AR.f32([128, 512]) for _ in range(2)]
        wq = AR.bf([128, 8, 3 * D])
        load_w(wq, wqkv_d, D, 3 * D, stg, "wst")
        l4 = AR.f32([128, 4, 64]); lt = AR.f32([128, 2, 64]); ls = AR.f32([128, 2]); nlam = AR.f32([128, 1])
        for i, d_ in enumerate((lq1_d, lk1_d, lq2_d, lk2_d)):
            dma(l4.ap[:, i, :], d_.partition_broadcast(128), [], [l4.b], "bc", allow_slow_non_contiguous=True)
        grp.append((l4.b, "bc"))
        wcol = AR.f32([128, 1])
        load_col(wcol, subw_d)
        fence()
        P.op("dve", lambda e: e.tensor_tensor(out=lt.ap[:, 0, :], in0=l4.ap[:, 0, :], in1=l4.ap[:, 1, :], op=ALU.mult),
             [l4.b], [lt.b])
        P.op("dve", lambda e: e.tensor_tensor(out=lt.ap[:, 1, :], in0=l4.ap[:, 2, :], in1=l4.ap[:, 3, :], op=ALU.mult),
             [l4.b], [lt.b])
        P.op("dve", lambda e: e.tensor_reduce(out=ls.ap, in_=lt.ap, axis=AX.X, op=ALU.add), [lt.b], [ls.b])
        P.op("act", lambda e: e.activation(out=ls.ap, in_=ls.ap, func=AF.Exp), [ls.b], [ls.b])
        P.op("dve", lambda e: e.tensor_tensor(out=nlam.ap, in0=ls.ap[:, 1:2], in1=ls.ap[:, 0:1], op=ALU.subtract),
             [ls.b], [nlam.b])
        P.op("dve", lambda e: e.tensor_scalar(out=nlam.ap, in0=nlam.ap, scalar1=-LAMBDA_INIT, scalar2=None, op0=ALU.add),
             [nlam.b], [nlam.b])
        P.op("dve", lambda e: e.tensor_scalar(out=wcol.ap, in0=wcol.ap, scalar1=1.0 - LAMBDA_INIT, scalar2=None,
                                              op0=ALU.mult), [wcol.b], [wcol.b])
        SW = min(512, S)
        Qh = AR.bf([128, S]); Kh = AR.bf([128, S]); Vh = AR.bf([128, NT, 128])
        h1s = [AR.bf([128, 8, SW]) for _ in range(2)]
        pts = [AR.bf([128, 2, SW]) for _ in range(3)]
        acc = [AR.f32([128, SW]) for _ in range(2)]
        rl = [AR.f32([128, SW]) for _ in range(2)]
        o1 = AR.f32([128, SW]); o2 = AR.f32([128, SW]); sq = AR.f32([128, SW]); rsd = AR.f32([128, SW])
        tmpr = AR.f32([128, SW])
        ofs = [AR.bf([128, SW]) for _ in range(2)]
        nq = S // SW
        KPB = SW // 128
        for h in range(AH):
            for su in range(nq):
                h1 = h1s[su % 2]
                dma(h1.ap, H1F[:, :, su * SW:(su + 1) * SW].rearrange("k p t -> p k t"), [DB("H1F", (su * SW) // 512)],
                    [h1.b], "h1%d" % (su % 2))
                for qk, dst in ((0, Qh), (1, Kh)):
                    ps = PB[qk]
                    c0 = qk * D + h * 128
                    for k in range(8):
                        P.op("pe", lambda e, k=k, ps=ps, c0=c0, h1=h1: e.matmul(
                            ps.ap[:, 0:SW], lhsT=wq.ap[:, k, c0:c0 + 128], rhs=h1.ap[:, k, :],
                            start=(k == 0), stop=(k == 7)), [wq.b, h1.b], [ps.b])
                    P.op("act" if qk == 0 else "dve",
                         (lambda e, ps=ps, dst=dst, su=su: e.activation(out=dst.ap[:, su * SW:(su + 1) * SW],
                                                                       in_=ps.ap[:, 0:SW], func=AF.Identity))
                         if qk == 0 else
                         (lambda e, ps=ps, dst=dst, su=su: e.tensor_copy(out=dst.ap[:, su * SW:(su + 1) * SW],
                                                                        in_=ps.ap[:, 0:SW])),
                         [ps.b], [dst.b])
                psv = PB[2]
                for tt in range(KPB):
                    for k in range(8):
                        P.op("pe", lambda e, k=k, tt=tt, h1=h1, h=h: e.matmul(
                            psv.ap[:, tt * 128:(tt + 1) * 128], lhsT=h1.ap[:, k, tt * 128:(tt + 1) * 128],
                            rhs=wq.ap[:, k, 2 * D + h * 128:2 * D + (h + 1) * 128], start=(k == 0), stop=(k == 7)),
                            [h1.b, wq.b], [psv.b])
                P.op("dve", lambda e, su=su: e.tensor_copy(
                    out=Vh.ap[:, su * KPB:(su + 1) * KPB, :],
                    in_=psv.ap[:, 0:SW].rearrange("p (t v) -> p t v", t=KPB)), [psv.b], [Vh.b])
            for qb in range(nq):
                pso = [PB[4], PB[5]]
                psl1 = PB[6]
                nk = (qb + 1) * KPB

                def c0_of(kt, qb=qb):
                    j = kt - qb * KPB
                    return 0 if j < 0 else 128 * j

                def st_S(kt, qb=qb):
                    c0 = c0_of(kt); r = kt % 2
                    for i in range(2):
                        pr = slice(64 * i, 64 * i + 64)
                        pb = PB[2 * r + i]
                        P.op("pe", lambda e, pr=pr, pb=pb, c0=c0, kt=kt: e.matmul(
                            pb.ap[:, c0:SW], lhsT=Kh.ap[pr, kt * 128:(kt + 1) * 128],
                            rhs=Qh.ap[pr, qb * SW + c0:(qb + 1) * SW], start=True, stop=True),
                            [Kh.b, Qh.b], [pb.b])

                def st_E(kt, qb=qb):
                    c0 = c0_of(kt); r = kt % 2
                    pt = pts[kt % 3]
                    src = psap[:, r * 1024:(r + 1) * 1024].rearrange("p (i q) -> p i q", i=2)
                    P.op("act", lambda e: e.activation(out=pt.ap[:, :, c0:SW], in_=src[:, :, c0:SW], func=AF.Exp,
                                                       scale=0.125), [PB[2 * r].b, PB[2 * r + 1].b], [pt.b])
                    if kt - qb * KPB >= 0:
                        P.op("pool", lambda e: e.memset(pt.ap[64:128, :, c0:c0 + 64], 0.0), [], [pt.b])
                    for i in range(1):
                        eng = "dve" if i == 0 else "pool"
                        if kt == 0:
                            P.op(eng, lambda e, i=i: e.tensor_copy(out=acc[i].ap, in_=pt.ap[:, i, :]), [pt.b], [acc[i].b])
                        else:
                            P.op(eng, lambda e, i=i: e.tensor_tensor(out=acc[i].ap[:, c0:SW], in0=acc[i].ap[:, c0:SW],
                                                                     in1=pt.ap[:, i, c0:SW], op=ALU.add),
                                 [pt.b, acc[i].b], [acc[i].b])

                def st_V(kt, pso=pso, nk=nk, psl1=psl1):
                    c0 = c0_of(kt)
                    pt = pts[kt % 3]
                    for i in range(2):
                        P.op("pe", lambda e, i=i: e.matmul(
                            pso[i].ap[:, c0:SW], lhsT=Vh.ap[:, kt, :], rhs=pt.ap[:, i, c0:SW],
                            start=(kt == 0), stop=(kt == nk - 1)), [Vh.b, pt.b], [pso[i].b])
                    P.op("pe", lambda e: e.matmul(psl1.ap[:, c0:SW], lhsT=ONESbf, rhs=pt.ap[:, 1, c0:SW],
                                                  start=(kt == 0), stop=(kt == nk - 1)), [CB, pt.b], [psl1.b])

                st_S(0)
                for kt in range(nk):
                    st_E(kt)
                    if kt + 1 < nk:
                        st_S(kt + 1)
                    st_V(kt)
                psl = [PB[0], psl1]
                P.op("pe", lambda e: e.matmul(psl[0].ap[:, 0:SW], lhsT=ONES32, rhs=acc[0].ap, start=True,
                                              stop=True), [acc[0].b, cst.b], [psl[0].b])
                for i in range(2):
                    P.op("dve", lambda e, i=i, psl=psl: e.reciprocal(out=rl[i].ap, in_=psl[i].ap[:, 0:SW]), [psl[i].b], [rl[i].b])
                P.op("dve", lambda e, pso=pso: e.tensor_tensor(out=o1.ap, in0=pso[0].ap[:, 0:SW], in1=rl[0].ap, op=ALU.mult),
                     [pso[0].b, rl[0].b], [o1.b])
                P.op("dve", lambda e, pso=pso: e.tensor_tensor(out=o2.ap, in0=pso[1].ap[:, 0:SW], in1=rl[1].ap, op=ALU.mult),
                     [pso[1].b, rl[1].b], [o2.b])
                P.op("dve", lambda e: e.scalar_tensor_tensor(out=o1.ap, in0=o2.ap, scalar=nlam.ap[:, 0:1], in1=o1.ap,
                                                             op0=ALU.mult, op1=ALU.add), [o2.b, o1.b, nlam.b], [o1.b])
                P.op("pool", lambda e: e.tensor_tensor(out=sq.ap, in0=o1.ap, in1=o1.ap, op=ALU.mult), [o1.b], [sq.b])
                pq = PB[0]
                P.op("pe", lambda e: e.matmul(pq.ap[:, 0:SW], lhsT=ONES32, rhs=sq.ap, start=True, stop=True),
                     [sq.b, cst.b], [pq.b])
                P.op("dve", lambda e: e.tensor_scalar(out=tmpr.ap, in0=pq.ap[:, 0:SW], scalar1=1.0 / 128, scalar2=EPS,
                                                      op0=ALU.mult, op1=ALU.add), [pq.b], [tmpr.b])
                P.op("act", lambda e: e.activation(out=tmpr.ap, in_=tmpr.ap, func=AF.Sqrt), [tmpr.b], [tmpr.b])
                P.op("dve", lambda e: e.reciprocal(out=rsd.ap, in_=tmpr.ap), [tmpr.b], [rsd.b])
                of = ofs[qb % 2]
                P.op("dve", lambda e, of=of: e.scalar_tensor_tensor(out=of.ap, in0=o1.ap, scalar=wcol.ap[:, 0:1],
                                                                    in1=rsd.ap, op0=ALU.mult, op1=ALU.mult),
                     [o1.b, wcol.b, rsd.b], [of.b])
                dma(OFs[h, :, qb * SW:(qb + 1) * SW], of.ap, [of.b], [DB("OFs", (qb * SW) // 512)],
                    "ofst%d" % (qb % 2), eng="pool")

    import os
    PH = os.environ.get("PHASES", "0AbBCdD")
    if "0" in PH:
        phase0()
        P.barrier()
    if "A" in PH:
        phaseA()
        P.barrier()
    if "b" in PH:
        tail1(0, YF, "YF", 16, wout_d, x_d, "xin", X1, "X1", H2F, "H2F")
        P.barrier()
    if "B" in PH:
        tail2(0, H2F, "H2F", X1, "X1", X2, "X2", H1F, "H1F")
        P.barrier()
    if "C" in PH:
        phaseC()
        P.barrier()
    if "d" in PH:
        tail1(1, OFs, "OFs", 8, awo_d, X2, "X2", X3, "X3", H4F, "H4F")
        P.barrier()
    if "D" in PH:
        tail2(1, H4F, "H4F", X3, "X3", out_d, "out")
        P.barrier()
    P.flush()
    es.close()
    return nc


def make_consts():
    j = np.arange(128)
    I = np.eye(128, dtype=np.float32)
    U = (j[:, None] <= j[None, :]).astype(np.float32)
    ones = np.ones((128, 128), np.float32)
    mb = np.where(j[None, :] < j[:, None], NEG, 0.0).astype(np.float32)
    mb8 = np.tile(mb[:, None, :], (1, 8, 1)).reshape(128, 1024)
    return np.ascontiguousarray(np.concatenate([I, U, ones, mb8], axis=1))


_cache = {}


def kernel(**inputs):
    x = np.asarray(inputs["x"], np.float32)
    B, S, _ = x.shape
    if S not in _cache:
        _cache[S] = build(S)
    nc = _cache[S]
    consts = make_consts()
    in_maps = []
    for core in range(8):
        b = core % B
        m = {"x": np.ascontiguousarray(x[b]), "c": np.ascontiguousarray(np.asarray(inputs["c"], np.float32)[b]),
             "consts": consts}
        for k in ("ada_w", "ada_b", "ln1_g", "ln1_b", "ln2_g", "ln2_b", "mlp_w1", "mlp_w2"):
            m[k] = np.ascontiguousarray(np.asarray(inputs[k], np.float32))
        for k in ("ssm_w_in", "ssm_conv_w", "ssm_conv_b", "ssm_dt_bias", "ssm_a_log", "ssm_d", "ssm_norm_w",
                  "ssm_w_out", "attn_w_qkv", "attn_lq1", "attn_lk1", "attn_lq2", "attn_lk2", "attn_subln_w",
                  "attn_w_out"):
            m[k] = np.ascontiguousarray(np.asarray(inputs[k], np.float32)[0])
        in_maps.append(m)
    res = run_bass_kernel_spmd(nc, in_maps, core_ids=list(range(8)))
    out = np.stack([np.asarray(res.results[b]["out"], np.float32) for b in range(B)], axis=0)
    return out
```

```python
import numpy as np, math
from contextlib import ExitStack
import concourse.bass as bass
import concourse.mybir as mybir
from concourse.bass_utils import run_bass_kernel_spmd

F32 = mybir.dt.float32
BF16 = mybir.dt.bfloat16
AF = mybir.ActivationFunctionType
ALU = mybir.AluOpType
AX = mybir.AxisListType

D = 1024
FF = 4096
DI = 2048
NH = 32
NGRP = 8
NST = 128
XBC = 4096
EPS = 1e-5
ALPHA = (2 * 2) ** 0.25
AH = 8
LAMBDA_INIT = 0.8 - 0.6 * math.exp(-0.3 * 1)
NEG = -30000.0
STRICT = True


class Ev:
    __slots__ = ("stream", "idx", "needed", "val", "sem")

    def __init__(self, stream, idx):
        self.stream = stream
        self.idx = idx
        self.needed = False
        self.val = None
        self.sem = None


class Buf:
    __slots__ = ("name", "w", "r")

    def __init__(self, name=""):
        self.name = name
        self.w = None
        self.r = {}


class T:
    __slots__ = ("ap", "b")

    def __init__(self, ap, b=None):
        self.ap = ap
        self.b = b if b is not None else Buf()


ENGS = ("pe", "act", "dve", "pool", "sp")


class Prog:
    def __init__(self, nc, es):
        self.nc = nc
        self.es = es
        self.streams = {e: [] for e in ENGS}
        self.cnt = {e: 0 for e in ENGS}
        self.esem = {e: es.enter_context(nc.semaphore("s_" + e)) for e in ENGS}
        self.dsem = {}
        self.dcnt = {}
        self.waited = {e: {} for e in ENGS}
        self.last = {}

    def op(self, eng, fn, reads=(), writes=(), dma=None):
        deps = []
        for b in reads:
            if b.w is not None:
                deps.append((b.w, 0))
        for b in writes:
            if b.w is not None:
                deps.append((b.w, 1))
            for ev in b.r.values():
                deps.append((ev, 2))
        waits = []
        wd = self.waited[eng]
        for ev, kind in deps:
            st = ev.stream
            if dma is None and st == eng and (eng == "pe" or (kind != 0 and not STRICT)):
                continue
            if wd.get(st, -1) >= ev.idx:
                continue
            wd[st] = ev.idx
            ev.needed = True
            waits.append(ev)
        if dma is not None:
            if dma not in self.dsem:
                self.dsem[dma] = self.es.enter_context(self.nc.semaphore("d_" + dma))
                self.dcnt[dma] = 0
            st = "D:" + dma
            self.dcnt[dma] += 1
            ev = Ev(st, self.dcnt[dma])
            ev.needed = True
        else:
            st = eng
            self.cnt[eng] += 1
            ev = Ev(st, self.cnt[eng])
        self.last[st] = ev
        self.streams[eng].append((waits, fn, ev, dma))
        for b in reads:
            b.r[st] = ev
        for b in writes:
            b.w = ev
            b.r = {}
        return ev

    def barrier(self):
        evs = list(self.last.values())
        for eng in ENGS:
            waits = []
            wd = self.waited[eng]
            for ev in evs:
                if ev.stream == eng:
                    continue
                if wd.get(ev.stream, -1) >= ev.idx:
                    continue
                wd[ev.stream] = ev.idx
                ev.needed = True
                waits.append(ev)
            if waits:
                self.streams[eng].append((waits, None, None, None))

    def flush(self):
        nc = self.nc
        for eng in ENGS:
            v = 0
            dv = {}
            for waits, fn, ev, dma in self.streams[eng]:
                if ev is None:
                    continue
                if dma is not None:
                    ev.val = 16 * ev.idx
                    ev.sem = self.dsem[dma]
                elif ev.needed:
                    v += 1
                    ev.val = v
                    ev.sem = self.esem[eng]
        streams = self.streams

        def emit(e, lst):
            for waits, fn, ev, dma in lst:
                for w in waits:
                    e.wait_ge(w.sem, w.val)
                if fn is None:
                    continue
                ins = fn(e)
                if dma is not None:
                    ins.then_inc(ev.sem, 16)
                elif ev.needed:
                    ins.then_inc(ev.sem, 1)

        with nc.Block() as block:
            @block.sync
            def _(e):
                emit(e, streams["sp"])

            @block.scalar
            def _(e):
                emit(e, streams["act"])

            @block.vector
            def _(e):
                emit(e, streams["dve"])

            @block.gpsimd
            def _(e):
                emit(e, streams["pool"])

            @block.tensor
            def _(e):
                emit(e, streams["pe"])


class Arena:
    def __init__(self, ap32):
        self.ap = ap32
        self.n = ap32.shape[1]
        self.off = 0

    def reset(self, off=0):
        self.off = off

    def f32(self, shape):
        n = int(np.prod(shape[1:]))
        a = self.ap[0:shape[0], self.off:self.off + n]
        self.off += n
        assert self.off <= self.n, ("SBUF arena overflow", self.off, self.n)
        if len(shape) == 3:
            a = a.rearrange("p (a b) -> p a b", a=shape[1])
        return T(a)

    def bf(self, shape):
        n = int(np.prod(shape[1:]))
        n32 = (n + 1) // 2
        a = self.ap[0:shape[0], self.off:self.off + n32].bitcast(BF16)[:, 0:n]
        self.off += n32
        assert self.off <= self.n, ("SBUF arena overflow", self.off, self.n)
        if len(shape) == 3:
            a = a.rearrange("p (a b) -> p a b", a=shape[1])
        return T(a)


def bc(ap2, n):
    return ap2.unsqueeze(2).to_broadcast([ap2.shape[0], ap2.shape[1], n])


def build(S, dbg=False):
    nc = bass.Bass("TRN2", target_bir_lowering=False)
    NT = S // 128
    es = ExitStack()
    ext = {}

    def din(name, shape):
        ext[name] = nc.dram_tensor(name, list(shape), F32, kind="ExternalInput").ap()
        return ext[name]

    x_d = din("x", [S, D])
    c_d = din("c", [D])
    adaw_d = din("ada_w", [2, D, 6 * D])
    adab_d = din("ada_b", [2, 6 * D])
    ln1g_d = din("ln1_g", [2, D]); ln1b_d = din("ln1_b", [2, D])
    ln2g_d = din("ln2_g", [2, D]); ln2b_d = din("ln2_b", [2, D])
    w1_d = din("mlp_w1", [2, D, FF]); w2_d = din("mlp_w2", [2, FF, D])
    win_d = din("ssm_w_in", [D, 6176])
    convw_d = din("ssm_conv_w", [4, XBC]); convb_d = din("ssm_conv_b", [XBC])
    dtb_d = din("ssm_dt_bias", [NH]); alog_d = din("ssm_a_log", [NH]); dsk_d = din("ssm_d", [NH])
    nw_d = din("ssm_norm_w", [DI]); wout_d = din("ssm_w_out", [DI, D])
    wqkv_d = din("attn_w_qkv", [D, 3 * D])
    lq1_d = din("attn_lq1", [64]); lk1_d = din("attn_lk1", [64])
    lq2_d = din("attn_lq2", [64]); lk2_d = din("attn_lk2", [64])
    subw_d = din("attn_subln_w", [128]); awo_d = din("attn_w_out", [D, D])
    cst_d = din("consts", [128, 3 * 128 + 1024])
    out_d = nc.dram_tensor("out", [S, D], F32, kind="ExternalOutput").ap()

    def scratch(name, shape, dt):
        kind = "ExternalOutput" if dbg else "Internal"
        return nc.dram_tensor(name, list(shape), dt, kind=kind).ap()

    MOD = scratch("MOD", [2, 6 * D], F32)
    YF = scratch("YF", [16, 128, S], BF16)
    X1 = scratch("X1", [S, D], F32)
    H2F = scratch("H2F", [8, 128, S], BF16)
    X2 = scratch("X2", [S, D], F32)
    H1F = scratch("H1F", [8, 128, S], BF16)
    OFs = scratch("OFs", [8, 128, S], BF16)
    X3 = scratch("X3", [S, D], F32)
    H4F = scratch("H4F", [8, 128, S], BF16)
    NSUP = max(1, S // 512)
    dbuf = {}

    def DB(name, i=0):
        k = (name, i)
        if k not in dbuf:
            dbuf[k] = Buf(name)
        return dbuf[k]

    big = es.enter_context(nc.sbuf_tensor("big", [128, 53000], F32))
    AR = Arena(big[:, :] if hasattr(big, "__getitem__") else big.ap())
    psum = es.enter_context(nc.psum_tensor("psum", [128, 4096], F32))
    psap = psum[:, :] if hasattr(psum, "__getitem__") else psum.ap()
    PB = [T(psap[:, i * 512:(i + 1) * 512]) for i in range(8)]
    P = Prog(nc, es)

    dq = [0]

    def dma(out, in_, reads, writes, key, eng="sp", **kw):
        P.op(eng, lambda e: e.dma_start(out=out, in_=in_, **kw), reads, writes, dma=key)

    cst = AR.f32([128, 3 * 128])
    dma(cst.ap, cst_d[:, 0:384], [], [cst.b], "cst")
    I32 = cst.ap[:, 0:128]
    U32 = cst.ap[:, 128:256]
    ONES32 = cst.ap[:, 256:384]
    cbf = AR.bf([128, 3 * 128 + 128 + 512])
    P.op("dve", lambda e: e.tensor_copy(out=cbf.ap[:, 0:384], in_=cst.ap[:, 0:384]), [cst.b], [cbf.b])
    P.op("dve", lambda e: e.tensor_scalar(out=cbf.ap[:, 384:512], in0=cst.ap[:, 128:256], scalar1=-1.0,
                                          scalar2=None, op0=ALU.mult), [cst.b], [cbf.b])
    _off = AR.off
    mtmp = AR.f32([128, 512])
    dma(mtmp.ap, cst_d[:, 384:896], [], [mtmp.b], "cst2")
    P.op("dve", lambda e: e.tensor_copy(out=cbf.ap[:, 512:512 + 512], in_=mtmp.ap), [mtmp.b], [cbf.b])
    AR.reset(_off)
    Ibf = cbf.ap[:, 0:128]; Ubf = cbf.ap[:, 128:256]; ONESbf = cbf.ap[:, 256:384]; NEGUbf = cbf.ap[:, 384:512]
    MASKbf = cbf.ap[:, 512:1024]
    CB = cbf.b
    base0 = AR.off
    P.barrier()

    def phase0():
        AR.reset(base0)
        craw = AR.f32([128, 8])
        cond = AR.f32([128, 8])
        dma(craw.ap, c_d.rearrange("(k p) -> p k", p=128), [], [craw.b], "c0", allow_slow_non_contiguous=True)
        P.op("act", lambda e: e.activation(out=cond.ap, in_=craw.ap, func=AF.Silu), [craw.b], [cond.b])
        wst = [AR.f32([128, 8, 512]) for _ in range(2)]
        brow = AR.f32([1, 6 * D])
        orow = AR.f32([1, 6 * D])
        n = 0
        for l in range(2):
            dma(brow.ap, adab_d[l:l + 1, :], [], [brow.b], "brow")
            for cb in range(12):
                w = wst[n % 2]
                dma(w.ap, adaw_d[l, :, cb * 512:(cb + 1) * 512].rearrange("(k p) n -> p k n", p=128),
                    [], [w.b], "adaw%d" % (n % 2))
                ps = PB[n % 2]
                for k in range(8):
                    P.op("pe", lambda e, ps=ps, w=w, k=k: e.matmul(ps.ap[0:1, :], lhsT=cond.ap[:, k:k + 1],
                                                                  rhs=w.ap[:, k, :], start=(k == 0), stop=(k == 7)),
                         [cond.b, w.b], [ps.b])
                P.op("dve", lambda e, ps=ps, cb=cb: e.tensor_tensor(out=orow.ap[:, cb * 512:(cb + 1) * 512],
                                                                    in0=ps.ap[0:1, :],
                                                                    in1=brow.ap[:, cb * 512:(cb + 1) * 512], op=ALU.add),
                     [ps.b, brow.b], [orow.b])
                n += 1
            dma(MOD[l:l + 1, :], orow.ap, [orow.b], [DB("MOD")], "modst")

    def load_w(dst, src, K, N, stg, tag, c0=0):
        nb = 0
        CH = stg[0].ap.shape[1]
        for k in range(K // 128):
            for cs in range(0, N, CH):
                cw = min(CH, N - cs)
                s = stg[nb % len(stg)]
                dma(s.ap[:, 0:cw], src[k * 128:(k + 1) * 128, cs:cs + cw], [], [s.b], "%s%d" % (tag, nb % len(stg)))
                eng = ("pool", "act", "pool")[nb % 3] if False else "pool"
                P.op(eng, lambda e, s=s, k=k, cs=cs, cw=cw: e.tensor_copy(out=dst.ap[:, k, c0 + cs:c0 + cs + cw],
                                                                       in_=s.ap[:, 0:cw]), [s.b], [dst.b])
                nb += 1

    grp = []

    def fence():
        for b, key in grp:
            b.w = P.last["D:" + key]
        del grp[:]

    def load_bc(dst, src_row, n):
        dma(dst.ap, src_row.partition_broadcast(128), [], [dst.b], "bc", allow_slow_non_contiguous=True)
        grp.append((dst.b, "bc"))

    def load_col(dst, src_row):
        dma(dst.ap, src_row.rearrange("(k p) -> p k", p=128), [], [dst.b], "col", allow_slow_non_contiguous=True)
        grp.append((dst.b, "col"))

    def rstd_from(var_ap, out_t, tmp_t, reads, scale=1.0):
        P.op("dve", lambda e: e.tensor_scalar(out=tmp_t.ap, in0=var_ap, scalar1=scale, scalar2=EPS,
                                              op0=ALU.mult, op1=ALU.add), reads, [tmp_t.b])
        P.op("act", lambda e: e.activation(out=tmp_t.ap, in_=tmp_t.ap, func=AF.Sqrt), [tmp_t.b], [tmp_t.b])
        P.op("dve", lambda e: e.reciprocal(out=out_t.ap, in_=tmp_t.ap), [tmp_t.b], [out_t.b])

    def tail1(layer, YFd, yname, KC, wo_src, xin_d, xin_name, X1d, x1name, H2d, h2name):
        AR.reset(base0)
        stg = [AR.f32([128, 512]) for _ in range(2)]
        wo = AR.bf([128, KC, D])
        load_w(wo, wo_src, KC * 128, D, stg, "wst")
        g_bc = AR.f32([128, D]); gam = AR.f32([128, D]); bet = AR.f32([128, D])
        load_bc(g_bc, MOD[layer, 2 * D:3 * D], D)
        load_bc(gam, ln1g_d[layer, :], D)
        load_bc(bet, ln1b_d[layer, :], D)
        sc = AR.f32([128, 8]); sh = AR.f32([128, 8])
        load_col(sc, MOD[layer, 4 * D:5 * D]); load_col(sh, MOD[layer, 3 * D:4 * D])
        fence()
        P.op("dve", lambda e: e.tensor_scalar(out=sc.ap, in0=sc.ap, scalar1=1.0, scalar2=None, op0=ALU.add),
             [sc.b], [sc.b])
        yts = [AR.bf([128, KC, 512]) for _ in range(2)]
        xts = [AR.f32([128, D]) for _ in range(2)]
        uts = [AR.f32([128, D]) for _ in range(2)]
        x1s = [AR.f32([128, D]) for _ in range(2)]
        hfs = [AR.bf([128, 8, 512]) for _ in range(2)]
        st6 = AR.f32([128, 12]); mv = AR.f32([128, 2]); rs = AR.f32([128, 1]); tmp1 = AR.f32([128, 1])
        for su in range(NSUP):
            SW = min(512, S)
            yt = yts[su % 2]
            dma(yt.ap[:, :, 0:SW], YFd[:, :, su * 512:su * 512 + SW].rearrange("k p t -> p k t"),
                [DB(yname, su)], [yt.b], "yt%d" % (su % 2))
            hf = hfs[su % 2]
            for tt in range(SW // 128):
                t = su * 4 + tt
                xt = xts[t % 2]; ut = uts[t % 2]; x1 = x1s[t % 2]
                dma(xt.ap, xin_d[t * 128:(t + 1) * 128, :], [DB(xin_name, su)], [xt.b], "xt%d" % (t % 2))
                pso = [PB[0 + 2 * (t % 2)], PB[1 + 2 * (t % 2)]]
                for hlf in range(2):
                    for k in range(KC):
                        P.op("pe", lambda e, hlf=hlf, k=k, yt=yt, tt=tt, pso=pso: e.matmul(
                            pso[hlf].ap, lhsT=yt.ap[:, k, tt * 128:(tt + 1) * 128],
                            rhs=wo.ap[:, k, hlf * 512:(hlf + 1) * 512], start=(k == 0), stop=(k == KC - 1)),
                            [yt.b, wo.b], [pso[hlf].b])
                for hlf in range(2):
                    P.op("dve", lambda e, hlf=hlf, pso=pso, ut=ut: e.tensor_tensor(
                        out=ut.ap[:, hlf * 512:(hlf + 1) * 512], in0=pso[hlf].ap,
                        in1=g_bc.ap[:, hlf * 512:(hlf + 1) * 512], op=ALU.mult), [pso[hlf].b, g_bc.b], [ut.b])
                ln_block(xt, ut, x1, gam, bet, st6, mv, rs, tmp1)
                dma(X1d[t * 128:(t + 1) * 128, :], x1.ap, [x1.b], [DB(x1name, su)], "x1st%d" % (t % 2), eng="pool")
                to_fm(x1, hf, tt, sc, sh)
            dma(H2d[:, :, su * 512:su * 512 + SW].rearrange("k p t -> p k t"), hf.ap[:, :, 0:SW], [hf.b],
                [DB(h2name, su)], "hfst%d" % (su % 2), eng="pool")

    def ln_block(xt, ut, x1, gam, bet, st6, mv, rs, tmp1):
        P.op("dve", lambda e: e.scalar_tensor_tensor(out=ut.ap, in0=xt.ap, scalar=ALPHA, in1=ut.ap,
                                                     op0=ALU.mult, op1=ALU.add), [xt.b, ut.b], [ut.b])
        P.op("dve", lambda e: e.bn_stats(out=st6.ap[:, 0:6], in_=ut.ap[:, 0:512]), [ut.b], [st6.b])
        P.op("dve", lambda e: e.bn_stats(out=st6.ap[:, 6:12], in_=ut.ap[:, 512:1024]), [ut.b], [st6.b])
        P.op("dve", lambda e: e.bn_aggr(out=mv.ap, in_=st6.ap), [st6.b], [mv.b])
        rstd_from(mv.ap[:, 1:2], rs, tmp1, [mv.b])
        P.op("dve", lambda e: e.tensor_scalar(out=x1.ap, in0=ut.ap, scalar1=mv.ap[:, 0:1], scalar2=rs.ap[:, 0:1],
                                              op0=ALU.subtract, op1=ALU.mult), [ut.b, mv.b, rs.b], [x1.b])
        P.op("pool", lambda e: e.tensor_tensor(out=x1.ap, in0=x1.ap, in1=gam.ap, op=ALU.mult), [x1.b, gam.b], [x1.b])
        P.op("pool", lambda e: e.tensor_tensor(out=x1.ap, in0=x1.ap, in1=bet.ap, op=ALU.add), [x1.b, bet.b], [x1.b])

    def to_fm(x1, hf, tt, sc, sh):
        for c in range(8):
            ps = PB[4 + (c % 4)]
            P.op("pe", lambda e, c=c, ps=ps: e.transpose(out=ps.ap[:, 0:128], in_=x1.ap[:, c * 128:(c + 1) * 128],
                                                         identity=I32), [x1.b, cst.b], [ps.b])
            P.op("act", lambda e, c=c, ps=ps: e.activation(out=hf.ap[:, c, tt * 128:(tt + 1) * 128], in_=ps.ap[:, 0:128],
                                                           func=AF.Identity, bias=sh.ap[:, c:c + 1],
                                                           scale=sc.ap[:, c:c + 1]), [ps.b, sc.b, sh.b], [hf.b])

    def tail2(layer, H2d, h2name, X1d, x1name, XOd, xoname, HNd=None, hnname=None):
        AR.reset(base0)
        stg = [AR.f32([128, 512]) for _ in range(2)]
        w1 = AR.bf([128, 8, FF]); w2 = AR.bf([128, 32, D])
        load_w(w1, w1_d[layer], D, FF, stg, "wst")
        load_w(w2, w2_d[layer], FF, D, stg, "wst")
        g_bc = AR.f32([128, D]); gam = AR.f32([128, D]); bet = AR.f32([128, D])
        load_bc(g_bc, MOD[layer, 5 * D:6 * D], D)
        load_bc(gam, ln2g_d[layer, :], D)
        load_bc(bet, ln2b_d[layer, :], D)
        sc = AR.f32([128, 8]); sh = AR.f32([128, 8])
        if HNd is not None:
            load_col(sc, MOD[layer + 1, 1 * D:2 * D]); load_col(sh, MOD[layer + 1, 0:D])
        fence()
        if HNd is not None:
            P.op("dve", lambda e: e.tensor_scalar(out=sc.ap, in0=sc.ap, scalar1=1.0, scalar2=None, op0=ALU.add),
                 [sc.b], [sc.b])
        SW = min(256, S)
        h2s = [AR.bf([128, 8, SW]) for _ in range(2)]
        hff = AR.bf([128, 32, SW])
        rts = [AR.f32([128, SW]) for _ in range(3)]
        xts = [AR.f32([128, D]) for _ in range(2)]
        uts = [AR.f32([128, D]) for _ in range(1)]
        x1s = [AR.f32([128, D]) for _ in range(2)]
        hfs = [AR.bf([128, 8, SW]) for _ in range(1)]
        st6 = AR.f32([128, 12]); mv = AR.f32([128, 2]); rs = AR.f32([128, 1]); tmp1 = AR.f32([128, 1])
        nsu = S // SW
        for su in range(nsu):
            dsu = (su * SW) // 512
            h2 = h2s[su % 2]
            dma(h2.ap, H2d[:, :, su * SW:(su + 1) * SW].rearrange("k p t -> p k t"), [DB(h2name, dsu)], [h2.b],
                "h2%d" % (su % 2))
            for fc in range(32):
                ps = PB[fc % 4]
                for k in range(8):
                    P.op("pe", lambda e, fc=fc, k=k, ps=ps, h2=h2: e.matmul(
                        ps.ap[:, 0:SW], lhsT=w1.ap[:, k, fc * 128:(fc + 1) * 128], rhs=h2.ap[:, k, :],
                        start=(k == 0), stop=(k == 7)), [w1.b, h2.b], [ps.b])
                rt = rts[fc % 3]
                P.op("act", lambda e, ps=ps, rt=rt: e.activation(out=rt.ap, in_=ps.ap[:, 0:SW], func=AF.Relu),
                     [ps.b], [rt.b])
                P.op("pool", lambda e, rt=rt, fc=fc: e.tensor_tensor(out=hff.ap[:, fc, :], in0=rt.ap, in1=rt.ap,
                                                                    op=ALU.mult), [rt.b], [hff.b])
            hf = hfs[0]
            for tt in range(SW // 128):
                t = su * (SW // 128) + tt
                xt = xts[t % 2]; ut = uts[0]; x1 = x1s[t % 2]
                dma(xt.ap, X1d[t * 128:(t + 1) * 128, :], [DB(x1name, dsu)], [xt.b], "xt%d" % (t % 2))
                pso = [PB[4 + 2 * (t % 2)], PB[5 + 2 * (t % 2)]]
                for hlf in range(2):
                    for fc in range(32):
                        P.op("pe", lambda e, hlf=hlf, fc=fc, tt=tt, pso=pso: e.matmul(
                            pso[hlf].ap, lhsT=hff.ap[:, fc, tt * 128:(tt + 1) * 128],
                            rhs=w2.ap[:, fc, hlf * 512:(hlf + 1) * 512], start=(fc == 0), stop=(fc == 31)),
                            [hff.b, w2.b], [pso[hlf].b])
                for hlf in range(2):
                    P.op("dve", lambda e, hlf=hlf, pso=pso, ut=ut: e.tensor_tensor(
                        out=ut.ap[:, hlf * 512:(hlf + 1) * 512], in0=pso[hlf].ap,
                        in1=g_bc.ap[:, hlf * 512:(hlf + 1) * 512], op=ALU.mult), [pso[hlf].b, g_bc.b], [ut.b])
                ln_block(xt, ut, x1, gam, bet, st6, mv, rs, tmp1)
                dma(XOd[t * 128:(t + 1) * 128, :], x1.ap, [x1.b], [DB(xoname, dsu)], "x1st%d" % (t % 2), eng="pool")
                if HNd is not None:
                    to_fm_pb(x1, hf, tt, sc, sh)
            if HNd is not None:
                dma(HNd[:, :, su * SW:(su + 1) * SW].rearrange("k p t -> p k t"), hf.ap, [hf.b],
                    [DB(hnname, dsu)], "hfst0", eng="pool")

    def to_fm_pb(x1, hf, tt, sc, sh):
        for c in range(8):
            ps = PB[c % 4]
            P.op("pe", lambda e, c=c, ps=ps: e.transpose(out=ps.ap[:, 256:384], in_=x1.ap[:, c * 128:(c + 1) * 128],
                                                         identity=I32), [x1.b, cst.b], [ps.b])
            P.op("act", lambda e, c=c, ps=ps: e.activation(out=hf.ap[:, c, tt * 128:(tt + 1) * 128],
                                                           in_=ps.ap[:, 256:384], func=AF.Identity,
                                                           bias=sh.ap[:, c:c + 1], scale=sc.ap[:, c:c + 1]),
                 [ps.b, sc.b, sh.b], [hf.b])

    def phaseA():
        AR.reset(base0)
        stg = [AR.f32([128, 512]) for _ in range(2)]
        win = AR.bf([128, 8, 6176])
        load_w(win, win_d, D, 6176, stg, "wst")
        sc = AR.f32([128, 8]); sh = AR.f32([128, 8])
        load_col(sc, MOD[0, D:2 * D]); load_col(sh, MOD[0, 0:D])
        cw = AR.f32([128, 4, 32])
        cbias = AR.f32([128, 32])
        for k in range(4):
            dma(cw.ap[:, k, :], convw_d[k, :].rearrange("(c p) -> p c", p=128), [], [cw.b], "col",
                allow_slow_non_contiguous=True)
        grp.append((cw.b, "col"))
        load_col(cbias, convb_d)
        dtb = AR.f32([128, NH]); Abc = AR.f32([128, NH]); Dbc = AR.f32([128, NH]); nwbc = AR.f32([128, DI])
        load_bc(dtb, dtb_d, NH); load_bc(Abc, alog_d, NH); load_bc(Dbc, dsk_d, NH); load_bc(nwbc, nw_d, DI)
        fence()
        P.op("dve", lambda e: e.tensor_scalar(out=sc.ap, in0=sc.ap, scalar1=1.0, scalar2=None, op0=ALU.add),
             [sc.b], [sc.b])
        P.op("act", lambda e: e.activation(out=Abc.ap, in_=Abc.ap, func=AF.Exp), [Abc.b], [Abc.b])
        P.op("dve", lambda e: e.tensor_scalar(out=Abc.ap, in0=Abc.ap, scalar1=-1.0, scalar2=None, op0=ALU.mult),
             [Abc.b], [Abc.b])
        halo = AR.f32([128, 32, 3])
        P.op("pool", lambda e: e.memset(halo.ap, 0.0), [], [halo.b])
        h32 = AR.f32([128, 8, 256]); hbf = AR.bf([128, 8, 256])
        P.op("pool", lambda e: e.memset(h32.ap, 0.0), [], [h32.b])
        P.op("pool", lambda e: e.memset(hbf.ap, 0.0), [], [hbf.b])
        SW = min(256, S)
        xts = [AR.f32([128, D]) for _ in range(2)]
        hF = AR.bf([128, 8, SW])
        raws = [AR.f32([128, SW + 3]) for _ in range(2)]
        accs = [AR.f32([128, SW]) for _ in range(2)]
        xbcF = AR.bf([128, 32, SW])
        dtr = AR.f32([128, NH]); dtt = AR.f32([128, NH]); a_t = AR.f32([128, NH]); a_bf = AR.bf([128, NH])
        eA = AR.f32([128, NH]); cdb = AR.f32([128, NH]); w2t = AR.f32([128, NH])
        P1 = AR.bf([128, 8, 128]); abc = AR.bf([128, 8, 128])
        LT = AR.bf([128, 8, 128]); MT = AR.bf([128, 8, 128])
        xTs = AR.bf([128, 512]); Xm = AR.bf([128, 512]); Xd = AR.bf([128, 512]); sk = AR.f32([128, 512])
        yo = AR.f32([128, 512]); yy = AR.f32([128, 512]); sz = AR.f32([128, 512]); gg = AR.f32([128, 512])
        junk = AR.f32([128, 256]); ss = AR.f32([128, 2]); rs2 = AR.f32([128, 2]); tmp2 = AR.f32([128, 2])
        yn = AR.bf([128, 512]); BTs = AR.bf([128, 2, 128]); htmp = AR.f32([128, 512])
        yfs = [AR.bf([128, 16, SW]) for _ in range(1)]
        for su in range(S // SW):
            for tt in range(SW // 128):
                t = su * (SW // 128) + tt
                xt = xts[t % 2]
                dma(xt.ap, x_d[t * 128:(t + 1) * 128, :], [], [xt.b], "xt%d" % (t % 2))
                to_fm(xt, hF, tt, sc, sh)
            for ct in range(32):
                ps = PB[ct % 2]
                col0 = DI + ct * 128
                for k in range(8):
                    P.op("pe", lambda e, k=k, ps=ps, col0=col0: e.matmul(
                        ps.ap[:, 0:SW], lhsT=win.ap[:, k, col0:col0 + 128], rhs=hF.ap[:, k, :],
                        start=(k == 0), stop=(k == 7)), [win.b, hF.b], [ps.b])
                raw = raws[ct % 2]; acc = accs[ct % 2]
                P.op("act", lambda e, ps=ps, raw=raw: e.activation(out=raw.ap[:, 3:3 + SW], in_=ps.ap[:, 0:SW],
                                                                  func=AF.Identity), [ps.b], [raw.b])
                P.op("pool", lambda e, raw=raw, ct=ct: e.tensor_copy(out=raw.ap[:, 0:3], in_=halo.ap[:, ct, :]),
                     [halo.b], [raw.b])
                P.op("pool", lambda e, raw=raw, ct=ct: e.tensor_copy(out=halo.ap[:, ct, :], in_=raw.ap[:, SW:SW + 3]),
                     [raw.b], [halo.b])
                P.op("dve", lambda e, raw=raw, acc=acc, ct=ct: e.tensor_scalar(
                    out=acc.ap, in0=raw.ap[:, 0:SW], scalar1=cw.ap[:, 0, ct:ct + 1], scalar2=cbias.ap[:, ct:ct + 1],
                    op0=ALU.mult, op1=ALU.add), [raw.b, cw.b, cbias.b], [acc.b])
                for k in range(1, 4):
                    P.op("dve", lambda e, raw=raw, acc=acc, ct=ct, k=k: e.scalar_tensor_tensor(
                        out=acc.ap, in0=raw.ap[:, k:k + SW], scalar=cw.ap[:, k, ct:ct + 1], in1=acc.ap,
                        op0=ALU.mult, op1=ALU.add), [raw.b, cw.b, acc.b], [acc.b])
                P.op("act", lambda e, acc=acc, ct=ct: e.activation(out=xbcF.ap[:, ct, :], in_=acc.ap, func=AF.Silu),
                     [acc.b], [xbcF.b])
            yf = yfs[0]
            for tt in range(SW // 128):
                tk = slice(tt * 128, (tt + 1) * 128)
                psm = PB[7]
                for k in range(8):
                    P.op("pe", lambda e, k=k, tk=tk: e.matmul(psm.ap[:, 0:NH], lhsT=hF.ap[:, k, tk],
                                                              rhs=win.ap[:, k, 6144:6176], start=(k == 0), stop=(k == 7)),
                         [hF.b, win.b], [psm.b])
                P.op("dve", lambda e: e.tensor_tensor(out=dtr.ap, in0=psm.ap[:, 0:NH], in1=dtb.ap, op=ALU.add),
                     [psm.b, dtb.b], [dtr.b])
                P.op("act", lambda e: e.activation(out=dtr.ap, in_=dtr.ap, func=AF.Exp), [dtr.b], [dtr.b])
                P.op("act", lambda e: e.activation(out=dtt.ap, in_=dtr.ap, func=AF.Ln, bias=1.0), [dtr.b], [dtt.b])
                P.op("dve", lambda e: e.tensor_tensor(out=a_t.ap, in0=dtt.ap, in1=Abc.ap, op=ALU.mult),
                     [dtt.b, Abc.b], [a_t.b])
                P.op("dve", lambda e: e.tensor_copy(out=a_bf.ap, in_=a_t.ap), [a_t.b], [a_bf.b])
                P.op("pe", lambda e: e.matmul(psm.ap[:, 32:64], lhsT=U32, rhs=a_t.ap, start=True, stop=True),
                     [a_t.b, cst.b], [psm.b])
                P.op("pe", lambda e: e.matmul(psm.ap[:, 64:96], lhsT=ONES32, rhs=a_t.ap, start=True, stop=True),
                     [a_t.b, cst.b], [psm.b])
                P.op("act", lambda e: e.activation(out=eA.ap, in_=psm.ap[:, 32:64], func=AF.Exp), [psm.b], [eA.b])
                P.op("act", lambda e: e.activation(out=cdb.ap, in_=psm.ap[:, 64:96], func=AF.Exp), [psm.b], [cdb.b])
                for gp in range(4):
                    hs = slice(gp * 8, gp * 8 + 8)
                    P.op("pool", lambda e, hs=hs: e.tensor_tensor(
                        out=P1.ap, in0=bc(a_bf.ap[:, hs], 128), in1=Ubf.unsqueeze(1).to_broadcast([128, 8, 128]),
                        op=ALU.mult), [a_bf.b, CB], [P1.b])
                    P.op("pool", lambda e, hs=hs: e.tensor_copy(out=abc.ap, in_=bc(a_bf.ap[:, hs], 128)),
                         [a_bf.b], [abc.b])
                    seg = [PB[2], PB[3]]
                    for hh in range(2):
                        cs = slice(hh * 4, hh * 4 + 4)
                        P.op("pe", lambda e, hh=hh, cs=cs: e.matmul(seg[hh].ap, lhsT=ONESbf, rhs=P1.ap[:, cs, :],
                                                                    start=True, stop=False), [P1.b, CB], [seg[hh].b])
                        P.op("pe", lambda e, hh=hh, cs=cs: e.matmul(seg[hh].ap, lhsT=NEGUbf, rhs=abc.ap[:, cs, :],
                                                                    start=False, stop=False), [abc.b, CB], [seg[hh].b])
                        P.op("pe", lambda e, hh=hh: e.matmul(seg[hh].ap, lhsT=Ibf,
                                                             rhs=MASKbf[:, 0:512],
                                                             start=False, stop=True), [CB], [seg[hh].b])
                        P.op("act", lambda e, hh=hh, cs=cs: e.activation(out=LT.ap[:, cs, :], in_=seg[hh].ap,
                                                                         func=AF.Exp), [seg[hh].b], [LT.b])
                    pcb = PB[4]
                    for g2 in range(2):
                        g = gp * 2 + g2
                        P.op("pe", lambda e, g=g, g2=g2, tk=tk: e.matmul(
                            pcb.ap[:, g2 * 128:(g2 + 1) * 128], lhsT=xbcF.ap[:, 16 + g, tk], rhs=xbcF.ap[:, 24 + g, tk],
                            start=True, stop=True), [xbcF.b], [pcb.b])
                    for g2 in range(2):
                        P.op("dve", lambda e, g2=g2: e.tensor_tensor(
                            out=MT.ap[:, g2 * 4:(g2 + 1) * 4, :], in0=LT.ap[:, g2 * 4:(g2 + 1) * 4, :],
                            in1=pcb.ap[:, g2 * 128:(g2 + 1) * 128].unsqueeze(1).to_broadcast([128, 4, 128]),
                            op=ALU.mult), [LT.b, pcb.b], [MT.b])
                    pxt = PB[5]
                    pxt_bf = pxt.ap.bitcast(BF16)
                    for c4 in range(4):
                        ct = gp * 4 + c4
                        P.op("pe", lambda e, ct=ct, c4=c4, tk=tk: e.transpose(
                            out=pxt_bf[:, c4 * 128:(c4 + 1) * 128], in_=xbcF.ap[:, ct, tk], identity=Ibf),
                            [xbcF.b, CB], [pxt.b])
                    P.op("act", lambda e: e.activation(out=xTs.ap, in_=pxt_bf[:, 0:512], func=AF.Identity),
                         [pxt.b], [xTs.b])
                    x3 = xTs.ap.rearrange("p (r d) -> p r d", r=8)
                    P.op("dve", lambda e, hs=hs: e.tensor_tensor(out=w2t.ap[:, hs], in0=dtt.ap[:, hs],
                                                                 in1=LT.ap[:, :, 127], op=ALU.mult),
                         [dtt.b, LT.b], [w2t.b])
                    P.op("pool", lambda e, hs=hs: e.tensor_tensor(
                        out=Xm.ap.rearrange("p (r d) -> p r d", r=8), in0=x3, in1=bc(dtt.ap[:, hs], 64), op=ALU.mult),
                        [xTs.b, dtt.b], [Xm.b])
                    P.op("pool", lambda e, hs=hs: e.tensor_tensor(
                        out=Xd.ap.rearrange("p (r d) -> p r d", r=8), in0=x3, in1=bc(w2t.ap[:, hs], 64), op=ALU.mult),
                        [xTs.b, w2t.b], [Xd.b])
                    P.op("pool", lambda e, hs=hs: e.tensor_tensor(
                        out=sk.ap.rearrange("p (r d) -> p r d", r=8), in0=x3, in1=bc(Dbc.ap[:, hs], 64), op=ALU.mult),
                        [xTs.b, Dbc.b], [sk.b])
                    pyd = PB[6]; pyo = PB[0]
                    for r in range(8):
                        P.op("pe", lambda e, r=r: e.matmul(pyd.ap[:, r * 64:(r + 1) * 64], lhsT=MT.ap[:, r, :],
                                                           rhs=Xm.ap[:, r * 64:(r + 1) * 64], start=True, stop=True),
                             [MT.b, Xm.b], [pyd.b])
                    for g2 in range(2):
                        g = gp * 2 + g2
                        P.op("pe", lambda e, g=g, g2=g2, tk=tk: e.matmul(
                            pyo.ap[:, g2 * 256:(g2 + 1) * 256], lhsT=xbcF.ap[:, 24 + g, tk], rhs=hbf.ap[:, g, :],
                            start=True, stop=True), [xbcF.b, hbf.b], [pyo.b])
                    P.op("dve", lambda e, hs=hs: e.tensor_tensor(
                        out=yo.ap.rearrange("p (r d) -> p r d", r=8), in0=pyo.ap.rearrange("p (r d) -> p r d", r=8),
                        in1=bc(eA.ap[:, hs], 64), op=ALU.mult), [pyo.b, eA.b], [yo.b])
                    P.op("pool", lambda e: e.tensor_tensor(out=yo.ap, in0=yo.ap, in1=sk.ap, op=ALU.add),
                         [yo.b, sk.b], [yo.b])
                    P.op("dve", lambda e: e.tensor_tensor(out=yy.ap, in0=pyd.ap, in1=yo.ap, op=ALU.add),
                         [pyd.b, yo.b], [yy.b])
                    pz = PB[1]
                    for k in range(8):
                        P.op("pe", lambda e, k=k, gp=gp, tk=tk: e.matmul(
                            pz.ap, lhsT=hF.ap[:, k, tk], rhs=win.ap[:, k, gp * 512:(gp + 1) * 512],
                            start=(k == 0), stop=(k == 7)), [hF.b, win.b], [pz.b])
                    P.op("act", lambda e: e.activation(out=sz.ap, in_=pz.ap, func=AF.Silu), [pz.b], [sz.b])
                    P.op("dve", lambda e: e.tensor_tensor(out=gg.ap, in0=yy.ap, in1=sz.ap, op=ALU.mult),
                         [yy.b, sz.b], [gg.b])
                    for g2 in range(2):
                        P.op("act", lambda e, g2=g2: e.activation(out=junk.ap, in_=gg.ap[:, g2 * 256:(g2 + 1) * 256],
                                                                  func=AF.Square, accum_out=ss.ap[:, g2:g2 + 1]),
                             [gg.b], [junk.b, ss.b])
                    rstd_from(ss.ap, rs2, tmp2, [ss.b], scale=1.0 / 256)
                    for g2 in range(2):
                        c0 = gp * 512 + g2 * 256
                        P.op("dve", lambda e, g2=g2, c0=c0: e.scalar_tensor_tensor(
                            out=yn.ap[:, g2 * 256:(g2 + 1) * 256], in0=gg.ap[:, g2 * 256:(g2 + 1) * 256],
                            scalar=rs2.ap[:, g2:g2 + 1], in1=nwbc.ap[:, c0:c0 + 256], op0=ALU.mult, op1=ALU.mult),
                            [gg.b, rs2.b, nwbc.b], [yn.b])
                    for c4 in range(4):
                        ct = gp * 4 + c4
                        P.op("pe", lambda e, c4=c4: e.transpose(out=pxt_bf[:, 512 + c4 * 128:512 + (c4 + 1) * 128],
                                                                in_=yn.ap[:, c4 * 128:(c4 + 1) * 128], identity=Ibf),
                             [yn.b, CB], [pxt.b])
                    P.op("act", lambda e, gp=gp, tk=tk, yf=yf: e.activation(
                        out=yf.ap[:, gp * 4:(gp + 1) * 4, tk],
                        in_=pxt_bf[:, 512:1024].rearrange("p (c t) -> p c t", c=4), func=AF.Identity),
                        [pxt.b], [yf.b])
                    pbt = PB[7]
                    pbt_bf = pbt.ap.bitcast(BF16)
                    for g2 in range(2):
                        g = gp * 2 + g2
                        P.op("pe", lambda e, g=g, g2=g2, tk=tk: e.transpose(
                            out=pbt_bf[:, 256 + g2 * 128:256 + (g2 + 1) * 128], in_=xbcF.ap[:, 16 + g, tk],
                            identity=Ibf), [xbcF.b, CB], [pbt.b])
                    P.op("act", lambda e: e.activation(out=BTs.ap, in_=pbt_bf[:, 256:512].rearrange(
                        "p (g n) -> p g n", g=2), func=AF.Identity), [pbt.b], [BTs.b])
                    pS = PB[0]
                    for g2 in range(2):
                        P.op("pe", lambda e, g2=g2: e.matmul(pS.ap[:, g2 * 256:(g2 + 1) * 256], lhsT=BTs.ap[:, g2, :],
                                                             rhs=Xd.ap[:, g2 * 256:(g2 + 1) * 256], start=True, stop=True),
                             [BTs.b, Xd.b], [pS.b])
                    hv = h32.ap[:, gp * 2:gp * 2 + 2, :].rearrange("p g (r d) -> p (g r) d", r=4)
                    P.op("pool", lambda e, hs=hs, hv=hv: e.tensor_tensor(
                        out=htmp.ap.rearrange("p (r d) -> p r d", r=8), in0=hv, in1=bc(cdb.ap[:, hs], 64), op=ALU.mult),
                        [h32.b, cdb.b], [htmp.b])
                    P.op("dve", lambda e, gp=gp: e.tensor_tensor(
                        out=h32.ap[:, gp * 2:gp * 2 + 2, :].rearrange("p g n -> p (g n)"), in0=pS.ap, in1=htmp.ap,
                        op=ALU.add), [pS.b, htmp.b], [h32.b])
                    P.op("act", lambda e, gp=gp: e.activation(out=hbf.ap[:, gp * 2:gp * 2 + 2, :],
                                                             in_=h32.ap[:, gp * 2:gp * 2 + 2, :], func=AF.Identity),
                         [h32.b], [hbf.b])
            dma(YF[:, :, su * SW:(su + 1) * SW].rearrange("k p t -> p k t"), yf.ap, [yf.b], [DB("YF", (su * SW) // 512)],
                "yfst0", eng="pool")

    def phaseC():
        AR.reset(base0)
        stg = [AR.f32([128, 512]) for _ in range(2)]
        wq = AR.bf([128, 8, 3 * D])
        load_w(wq, wqkv_d, D, 3 * D, stg, "wst")
        l4 = AR.f32([128, 4, 64]); lt = AR.f32([128, 2, 64]); ls = AR.f32([128, 2]); nlam = AR.f32([128, 1])
        for i, d_ in enumerate((lq1_d, lk1_d, lq2_d, lk2_d)):
            dma(l4.ap[:, i, :], d_.partition_broadcast(128), [], [l4.b], "bc", allow_slow_non_contiguous=True)
        grp.append((l4.b, "bc"))
        wcol = AR.f32([128, 1])
        load_col(wcol, subw_d)
        fence()
        P.op("dve", lambda e: e.tensor_tensor(out=lt.ap[:, 0, :], in0=l4.ap[:, 0, :], in1=l4.ap[:, 1, :], op=ALU.mult),
             [l4.b], [lt.b])
        P.op("dve", lambda e: e.tensor_tensor(out=lt.ap[:, 1, :], in0=l4.ap[:, 2, :], in1=l4.ap[:, 3, :], op=ALU.mult),
             [l4.b], [lt.b])
        P.op("dve", lambda e: e.tensor_reduce(out=ls.ap, in_=lt.ap, axis=AX.X, op=ALU.add), [lt.b], [ls.b])
        P.op("act", lambda e: e.activation(out=ls.ap, in_=ls.ap, func=AF.Exp), [ls.b], [ls.b])
        P.op("dve", lambda e: e.tensor_tensor(out=nlam.ap, in0=ls.ap[:, 1:2], in1=ls.ap[:, 0:1], op=ALU.subtract),
             [ls.b], [nlam.b])
        P.op("dve", lambda e: e.tensor_scalar(out=nlam.ap, in0=nlam.ap, scalar1=-LAMBDA_INIT, scalar2=None, op0=ALU.add),
             [nlam.b], [nlam.b])
        P.op("dve", lambda e: e.tensor_scalar(out=wcol.ap, in0=wcol.ap, scalar1=1.0 - LAMBDA_INIT, scalar2=None,
                                              op0=ALU.mult), [wcol.b], [wcol.b])
        SW = min(512, S)
        Qh = AR.bf([128, S]); Kh = AR.bf([128, S]); Vh = AR.bf([128, NT, 128])
        h1s = [AR.bf([128, 8, SW]) for _ in range(2)]
        pts = [AR.bf([128, 2, SW]) for _ in range(3)]
        acc = [AR.f32([128, SW]) for _ in range(2)]
        rl = [AR.f32([128, SW]) for _ in range(2)]
        o1 = AR.f32([128, SW]); o2 = AR.f32([128, SW]); sq = AR.f32([128, SW]); rsd = AR.f32([128, SW])
        tmpr = AR.f32([128, SW])
        ofs = [AR.bf([128, SW]) for _ in range(2)]
        nq = S // SW
        KPB = SW // 128
        for h in range(AH):
            for su in range(nq):
                h1 = h1s[su % 2]
                dma(h1.ap, H1F[:, :, su * SW:(su + 1) * SW].rearrange("k p t -> p k t"), [DB("H1F", (su * SW) // 512)],
                    [h1.b], "h1%d" % (su % 2))
                for qk, dst in ((0, Qh), (1, Kh)):
                    ps = PB[qk]
                    c0 = qk * D + h * 128
                    for k in range(8):
                        P.op("pe", lambda e, k=k, ps=ps, c0=c0, h1=h1: e.matmul(
                            ps.ap[:, 0:SW], lhsT=wq.ap[:, k, c0:c0 + 128], rhs=h1.ap[:, k, :],
                            start=(k == 0), stop=(k == 7)), [wq.b, h1.b], [ps.b])
                    P.op("act" if qk == 0 else "dve",
                         (lambda e, ps=ps, dst=dst, su=su: e.activation(out=dst.ap[:, su * SW:(su + 1) * SW],
                                                                       in_=ps.ap[:, 0:SW], func=AF.Identity))
                         if qk == 0 else
                         (lambda e, ps=ps, dst=dst, su=su: e.tensor_copy(out=dst.ap[:, su * SW:(su + 1) * SW],
                                                                        in_=ps.ap[:, 0:SW])),
                         [ps.b], [dst.b])
                psv = PB[2]
                for tt in range(KPB):
                    for k in range(8):
                        P.op("pe", lambda e, k=k, tt=tt, h1=h1, h=h: e.matmul(
                            psv.ap[:, tt * 128:(tt + 1) * 128], lhsT=h1.ap[:, k, tt * 128:(tt + 1) * 128],
                            rhs=wq.ap[:, k, 2 * D + h * 128:2 * D + (h + 1) * 128], start=(k == 0), stop=(k == 7)),
                            [h1.b, wq.b], [psv.b])
                P.op("dve", lambda e, su=su: e.tensor_copy(
                    out=Vh.ap[:, su * KPB:(su + 1) * KPB, :],
                    in_=psv.ap[:, 0:SW].rearrange("p (t v) -> p t v", t=KPB)), [psv.b], [Vh.b])
            for qb in range(nq):
                pso = [PB[4], PB[5]]
                psl1 = PB[6]
                nk = (qb + 1) * KPB

                def c0_of(kt, qb=qb):
                    j = kt - qb * KPB
                    return 0 if j < 0 else 128 * j

                def st_S(kt, qb=qb):
                    c0 = c0_of(kt); r = kt % 2
                    for i in range(2):
                        pr = slice(64 * i, 64 * i + 64)
                        pb = PB[2 * r + i]
                        P.op("pe", lambda e, pr=pr, pb=pb, c0=c0, kt=kt: e.matmul(
                            pb.ap[:, c0:SW], lhsT=Kh.ap[pr, kt * 128:(kt + 1) * 128],
                            rhs=Qh.ap[pr, qb * SW + c0:(qb + 1) * SW], start=True, stop=True),
                            [Kh.b, Qh.b], [pb.b])

                def st_E(kt, qb=qb):
                    c0 = c0_of(kt); r = kt % 2
                    pt = pts[kt % 3]
                    src = psap[:, r * 1024:(r + 1) * 1024].rearrange("p (i q) -> p i q", i=2)
                    P.op("act", lambda e: e.activation(out=pt.ap[:, :, c0:SW], in_=src[:, :, c0:SW], func=AF.Exp,
                                                       scale=0.125), [PB[2 * r].b, PB[2 * r + 1].b], [pt.b])
                    if kt - qb * KPB >= 0:
                        P.op("pool", lambda e: e.memset(pt.ap[64:128, :, c0:c0 + 64], 0.0), [], [pt.b])
                    for i in range(1):
                        eng = "dve" if i == 0 else "pool"
                        if kt == 0:
                            P.op(eng, lambda e, i=i: e.tensor_copy(out=acc[i].ap, in_=pt.ap[:, i, :]), [pt.b], [acc[i].b])
                        else:
                            P.op(eng, lambda e, i=i: e.tensor_tensor(out=acc[i].ap[:, c0:SW], in0=acc[i].ap[:, c0:SW],
                                                                     in1=pt.ap[:, i, c0:SW], op=ALU.add),
                                 [pt.b, acc[i].b], [acc[i].b])

                def st_V(kt, pso=pso, nk=nk, psl1=psl1):
                    c0 = c0_of(kt)
                    pt = pts[kt % 3]
                    for i in range(2):
                        P.op("pe", lambda e, i=i: e.matmul(
                            pso[i].ap[:, c0:SW], lhsT=Vh.ap[:, kt, :], rhs=pt.ap[:, i, c0:SW],
                            start=(kt == 0), stop=(kt == nk - 1)), [Vh.b, pt.b], [pso[i].b])
                    P.op("pe", lambda e: e.matmul(psl1.ap[:, c0:SW], lhsT=ONESbf, rhs=pt.ap[:, 1, c0:SW],
                                                  start=(kt == 0), stop=(kt == nk - 1)), [CB, pt.b], [psl1.b])

                st_S(0)
                if nk > 1:
                    st_S(1)
                for kt in range(nk):
                    st_E(kt)
                    if kt + 2 < nk:
                        st_S(kt + 2)
                    st_V(kt)
                psl = [PB[0], psl1]
                P.op("pe", lambda e: e.matmul(psl[0].ap[:, 0:SW], lhsT=ONES32, rhs=acc[0].ap, start=True,
                                              stop=True), [acc[0].b, cst.b], [psl[0].b])
                for i in range(2):
                    P.op("dve", lambda e, i=i, psl=psl: e.reciprocal(out=rl[i].ap, in_=psl[i].ap[:, 0:SW]), [psl[i].b], [rl[i].b])
                P.op("dve", lambda e, pso=pso: e.tensor_tensor(out=o1.ap, in0=pso[0].ap[:, 0:SW], in1=rl[0].ap, op=ALU.mult),
                     [pso[0].b, rl[0].b], [o1.b])
                P.op("dve", lambda e, pso=pso: e.tensor_tensor(out=o2.ap, in0=pso[1].ap[:, 0:SW], in1=rl[1].ap, op=ALU.mult),
                     [pso[1].b, rl[1].b], [o2.b])
                P.op("dve", lambda e: e.scalar_tensor_tensor(out=o1.ap, in0=o2.ap, scalar=nlam.ap[:, 0:1], in1=o1.ap,
                                                             op0=ALU.mult, op1=ALU.add), [o2.b, o1.b, nlam.b], [o1.b])
                P.op("pool", lambda e: e.tensor_tensor(out=sq.ap, in0=o1.ap, in1=o1.ap, op=ALU.mult), [o1.b], [sq.b])
                pq = PB[0]
                P.op("pe", lambda e: e.matmul(pq.ap[:, 0:SW], lhsT=ONES32, rhs=sq.ap, start=True, stop=True),
                     [sq.b, cst.b], [pq.b])
                P.op("dve", lambda e: e.tensor_scalar(out=tmpr.ap, in0=pq.ap[:, 0:SW], scalar1=1.0 / 128, scalar2=EPS,
                                                      op0=ALU.mult, op1=ALU.add), [pq.b], [tmpr.b])
                P.op("act", lambda e: e.activation(out=tmpr.ap, in_=tmpr.ap, func=AF.Sqrt), [tmpr.b], [tmpr.b])
                P.op("dve", lambda e: e.reciprocal(out=rsd.ap, in_=tmpr.ap), [tmpr.b], [rsd.b])
                of = ofs[qb % 2]
                P.op("dve", lambda e, of=of: e.scalar_tensor_tensor(out=of.ap, in0=o1.ap, scalar=wcol.ap[:, 0:1],
                                                                    in1=rsd.ap, op0=ALU.mult, op1=ALU.mult),
                     [o1.b, wcol.b, rsd.b], [of.b])
                dma(OFs[h, :, qb * SW:(qb + 1) * SW], of.ap, [of.b], [DB("OFs", (qb * SW) // 512)],
                    "ofst%d" % (qb % 2), eng="pool")

    import os
    PH = os.environ.get("PHASES", "0AbBCdD")
    if "0" in PH:
        phase0()
        P.barrier()
    if "A" in PH:
        phaseA()
        P.barrier()
    if "b" in PH:
        tail1(0, YF, "YF", 16, wout_d, x_d, "xin", X1, "X1", H2F, "H2F")
        P.barrier()
    if "B" in PH:
        tail2(0, H2F, "H2F", X1, "X1", X2, "X2", H1F, "H1F")
        P.barrier()
    if "C" in PH:
        phaseC()
        P.barrier()
    if "d" in PH:
        tail1(1, OFs, "OFs", 8, awo_d, X2, "X2", X3, "X3", H4F, "H4F")
        P.barrier()
    if "D" in PH:
        tail2(1, H4F, "H4F", X3, "X3", out_d, "out")
        P.barrier()
    P.flush()
    es.close()
    return nc


def make_consts():
    j = np.arange(128)
    I = np.eye(128, dtype=np.float32)
    U = (j[:, None] <= j[None, :]).astype(np.float32)
    ones = np.ones((128, 128), np.float32)
    mb = np.where(j[None, :] < j[:, None], NEG, 0.0).astype(np.float32)
    mb8 = np.tile(mb[:, None, :], (1, 8, 1)).reshape(128, 1024)
    return np.ascontiguousarray(np.concatenate([I, U, ones, mb8], axis=1))


_cache = {}


def kernel(**inputs):
    x = np.asarray(inputs["x"], np.float32)
    B, S, _ = x.shape
    if S not in _cache:
        _cache[S] = build(S)
    nc = _cache[S]
    consts = make_consts()
    in_maps = []
    for core in range(8):
        b = core % B
        m = {"x": np.ascontiguousarray(x[b]), "c": np.ascontiguousarray(np.asarray(inputs["c"], np.float32)[b]),
             "consts": consts}
        for k in ("ada_w", "ada_b", "ln1_g", "ln1_b", "ln2_g", "ln2_b", "mlp_w1", "mlp_w2"):
            m[k] = np.ascontiguousarray(np.asarray(inputs[k], np.float32))
        for k in ("ssm_w_in", "ssm_conv_w", "ssm_conv_b", "ssm_dt_bias", "ssm_a_log", "ssm_d", "ssm_norm_w",
                  "ssm_w_out", "attn_w_qkv", "attn_lq1", "attn_lk1", "attn_lq2", "attn_lk2", "attn_subln_w",
                  "attn_w_out"):
            m[k] = np.ascontiguousarray(np.asarray(inputs[k], np.float32)[0])
        in_maps.append(m)
    res = run_bass_kernel_spmd(nc, in_maps, core_ids=list(range(8)))
    out = np.stack([np.asarray(res.results[b]["out"], np.float32) for b in range(B)], axis=0)
    return out
```
